# Optimizing a Trainium2 kernel written in Bass

```python
import jax, jax.numpy as jnp
from jax import lax
import numpy as np

D_MODEL = 2048
BATCH = 1
SEQ = 16384
DEPTH = 1

CTX_LEN = 256
GRID_W = 64
A_HEADS = 8
A_DK = 128
A_DV = 128
A_WIDTH = A_HEADS * A_DK
B_WIDTH = 1024
CONV_W = 3
D_FF = 5632
CHUNK = 64
EPS = 1e-6
REC_COLS = 4 * A_WIDTH
IN_COLS = REC_COLS + A_WIDTH + 3 * B_WIDTH + 2 * D_MODEL

kernel_name = "hgrn2_shortconv_gated_hybrid_dit"


def rms_norm(x, g):
    x32 = x.astype(jnp.float32)
    y = x32 * lax.rsqrt(jnp.mean(x32 * x32, axis=-1, keepdims=True) + EPS)
    return (y * g.astype(jnp.float32)).astype(x.dtype)


def modulate(h, shift, scale):
    return h * (1 + scale) + shift


def short_conv_1d(u, w):
    up = jnp.pad(u, ((0, 0), (1, 1), (0, 0)))
    return w[0] * up[:, :-2] + w[1] * up[:, 1:-1] + w[2] * up[:, 2:]


def dwconv_grid(u, w, bias):
    bn, L, F = u.shape
    rows = L // GRID_W
    u4 = u.reshape(bn, rows, GRID_W, F)
    y = lax.conv_general_dilated(u4, w[:, :, None, :].astype(u.dtype), (1, 1), 'SAME',
                                 dimension_numbers=('NHWC', 'HWIO', 'NHWC'), feature_group_count=F)
    return y.reshape(bn, L, F) + bias


def hgrn2_chunk_scan(q, k, v, logf, s0):
    bn, L, H, _ = q.shape
    nc = L // CHUNK

    def to_chunks(t):
        return jnp.moveaxis(t.reshape(bn, nc, CHUNK, H, t.shape[-1]), 1, 0)

    causal = jnp.tril(jnp.ones((CHUNK, CHUNK), dtype=bool))[None, :, :, None, None]

    def step(S, inp):
        qc, kc, vc, gc = inp
        b = jnp.cumsum(gc, axis=1)
        o_inter = jnp.einsum('bchk,bhkv->bchv', qc * jnp.exp(b), S)
        decay = jnp.exp(jnp.where(causal, b[:, :, None] - b[:, None, :], -jnp.inf))
        scores = jnp.einsum('bthk,bshk,btshk->btsh', qc, kc, decay)
        o_intra = jnp.einsum('btsh,bshv->bthv', scores, vc)
        b_last = b[:, -1]
        k_dec = kc * jnp.exp(b_last[:, None] - b)
        S = jnp.exp(b_last)[..., None] * S + jnp.einsum('bshk,bshv->bhkv', k_dec, vc)
        return S, o_inter + o_intra

    s_fin, o = lax.scan(step, s0, (to_chunks(q), to_chunks(k), to_chunks(v), to_chunks(logf)))
    o = jnp.moveaxis(o, 0, 1).reshape(bn, L, H, v.shape[-1])
    return o, s_fin


def token_mixer(h, w_in, lb_f, lb_b, a_norm_g, sconv_w, w_pa, w_pb, w_o, s0_f, s0_b, need_output):
    bn, L, _ = h.shape
    heads = lambda t: t.reshape(bn, L, A_HEADS, -1)
    z_rec = h @ w_in[:, :REC_COLS]
    q, zf, zb, v = jnp.split(z_rec, 4, axis=-1)
    q = heads(jax.nn.silu(q.astype(jnp.float32)))
    v = heads(v.astype(jnp.float32))

    def direction(zg, lb, s0, flip):
        f = lb + (1 - lb) * jax.nn.sigmoid(zg.astype(jnp.float32))
        qq, kk, vv, lf = q, heads(1 - f), v, heads(jnp.log(f))
        if flip:
            qq, kk, vv, lf = (jnp.flip(t, axis=1) for t in (qq, kk, vv, lf))
        o, s = hgrn2_chunk_scan(qq, kk, vv, lf, s0)
        return (jnp.flip(o, axis=1) if flip else o), s

    o_f, s_f = direction(zf, lb_f, s0_f, False)
    o_b, s_b = direction(zb, lb_b, s0_b, True)
    if not need_output:
        return None, s_f, s_b

    z_rest = h @ w_in[:, REC_COLS:]
    g, sb, sc, sh, ga, gb = jnp.split(
        z_rest, np.cumsum([A_WIDTH, B_WIDTH, B_WIDTH, B_WIDTH, D_MODEL])[:].tolist(), axis=-1)
    o = o_f + o_b
    o = o * lax.rsqrt(jnp.mean(o * o, axis=-1, keepdims=True) + EPS) * a_norm_g.astype(jnp.float32)
    y_a = (o.reshape(bn, L, A_WIDTH) * jax.nn.silu(g.astype(jnp.float32))).astype(h.dtype)
    y_b = sb * short_conv_1d(sc * sh, sconv_w)
    merged = jax.nn.sigmoid(ga) * (y_a @ w_pa) + jax.nn.sigmoid(gb) * (y_b @ w_pb)
    return merged @ w_o, s_f, s_b


def conv_ffn(h, w_up, dw, db, w_down, on_grid):
    a, b = jnp.split(h @ w_up, 2, axis=-1)
    a = dwconv_grid(a, dw, db) if on_grid else short_conv_1d(a, dw[1]) + db
    return (jax.nn.silu(a) * b) @ w_down


def setup_inputs(seed: int = 0) -> dict:
    key = jax.random.key(seed)
    ks = jax.random.split(key, 20)
    nrm = lambda k, shape, s: jax.random.normal(k, shape, jnp.float32) * s
    D = D_MODEL
    return {
        "x": nrm(ks[0], (BATCH, SEQ, D), 1.0),
        "c": nrm(ks[1], (BATCH, D), 1.0),
        "ctx": nrm(ks[2], (BATCH, CTX_LEN, D), 1.0),
        "c_ctx": nrm(ks[3], (D,), 1.0),
        "w_mod": nrm(ks[4], (DEPTH, D, 6 * D), 0.5 * D ** -0.5),
        "b_mod": nrm(ks[5], (DEPTH, 6 * D), 0.02),
        "norm1_g": 1.0 + nrm(ks[6], (DEPTH, D), 0.02),
        "w_in": nrm(ks[7], (DEPTH, D, IN_COLS), D ** -0.5),
        "lb_raw": nrm(ks[8], (DEPTH + 1, 2, A_WIDTH), 0.5),
        "a_norm_g": 1.0 + nrm(ks[9], (DEPTH, A_DV), 0.02),
        "sconv_w": nrm(ks[10], (DEPTH, CONV_W, B_WIDTH), CONV_W ** -0.5),
        "w_pa": nrm(ks[11], (DEPTH, A_WIDTH, D), A_WIDTH ** -0.5),
        "w_pb": nrm(ks[12], (DEPTH, B_WIDTH, D), B_WIDTH ** -0.5),
        "w_o": nrm(ks[13], (DEPTH, D, D), D ** -0.5),
        "norm2_g": 1.0 + nrm(ks[14], (DEPTH, D), 0.02),
        "w_up": nrm(ks[15], (DEPTH, D, 2 * D_FF), D ** -0.5),
        "ffn_dw": nrm(ks[16], (DEPTH, 3, 3, D_FF), 1.0 / 3.0),
        "ffn_db": nrm(ks[17], (DEPTH, D_FF), 0.02),
        "w_down": nrm(ks[18], (DEPTH, D_FF, D), D_FF ** -0.5),
        "final_g": 1.0 + nrm(ks[19], (D,), 0.02),
    }


def reference(x, c, ctx, c_ctx, w_mod, b_mod, norm1_g, w_in, lb_raw, a_norm_g, sconv_w,
              w_pa, w_pb, w_o, norm2_g, w_up, ffn_dw, ffn_db, w_down, final_g):
    bn = x.shape[0]
    D = D_MODEL
    lbs = jnp.cumsum(jax.nn.softmax(lb_raw.astype(jnp.float32), axis=0), axis=0)
    zero_state = jnp.zeros((bn, A_HEADS, A_DK, A_DV), jnp.float32)
    silu_c = jax.nn.silu(c)
    silu_cc = jax.nn.silu(c_ctx)
    h_ctx = ctx
    for l in range(DEPTH):
        last = l == DEPTH - 1
        mod = silu_c @ w_mod[l] + b_mod[l]
        sh1, sc1, g1, sh2, sc2, g2 = (t[:, None, :] for t in jnp.split(mod, 6, axis=-1))
        n_ctx_mod = 2 if last else 6
        mod_c = silu_cc @ w_mod[l][:, :n_ctx_mod * D] + b_mod[l][:n_ctx_mod * D]
        mc = jnp.split(mod_c, n_ctx_mod)
        mix_args = (w_in[l], lbs[l, 0], lbs[l, 1], a_norm_g[l], sconv_w[l], w_pa[l], w_pb[l], w_o[l])
        hc = modulate(rms_norm(h_ctx, norm1_g[l]), mc[0], mc[1])
        out_c, s_f, s_b = token_mixer(hc, *mix_args, zero_state, zero_state, not last)
        hx = modulate(rms_norm(x, norm1_g[l]), sh1, sc1)
        out_x, _, _ = token_mixer(hx, *mix_args, s_f, s_b, True)
        x = x + g1 * out_x
        x = x + g2 * conv_ffn(modulate(rms_norm(x, norm2_g[l]), sh2, sc2),
                              w_up[l], ffn_dw[l], ffn_db[l], w_down[l], True)
        if not last:
            h_ctx = h_ctx + mc[2] * out_c
            h_ctx = h_ctx + mc[5] * conv_ffn(modulate(rms_norm(h_ctx, norm2_g[l]), mc[3], mc[4]),
                                             w_up[l], ffn_dw[l], ffn_db[l], w_down[l], False)
    return rms_norm(x, final_g)
```

```python
import numpy as np
from contextlib import ExitStack
import concourse.bass as bass
import concourse.mybir as mybir
from concourse.bass_utils import run_bass_kernel_spmd

F32 = mybir.dt.float32
BF16 = mybir.dt.bfloat16
AF = mybir.ActivationFunctionType
ALU = mybir.AluOpType

D = 2048
KD = 16
AW = 1024
NH = 8
DFF = 5632
NFC = 44
INC = 12288
EPS = 1e-6
NCORES = 8
CTXL = 256


class Ctr:
    def __init__(self, h, name):
        self.h = h
        self.n = 0
        self.name = name


class Buf:
    __slots__ = ("name", "w", "r")

    def __init__(self, name=""):
        self.name = name
        self.w = None
        self.r = {}


class K:
    ENG = ("pe", "act", "dve", "pool", "sp")

    def __init__(self, nc, es):
        self.nc = nc
        self.es = es
        self.prog = {e: [] for e in self.ENG}
        self.ctr = {e: Ctr(nc.alloc_semaphore(name="c_" + e), e) for e in self.ENG}
        self.seen = {e: {} for e in self.ENG}
        self.allctr = list(self.ctr.values())
        self.uid = 0
        self.dead = False

    def sb(self, name, shape, dt, es=None):
        self.uid += 1
        return (es or self.es).enter_context(self.nc.sbuf_tensor(f"s{self.uid}_{name}", list(shape), dt))

    def ps(self, name, shape, dt=F32):
        return self.es.enter_context(self.nc.psum_tensor("p_" + name, list(shape), dt))

    def newctr(self, name):
        c = Ctr(self.nc.alloc_semaphore(name=name), name)
        self.allctr.append(c)
        return c

    def _wait(self, e, tk):
        if tk is None:
            return
        c, v = tk
        if c.name not in self.ENG:
            v = c.n
        s = self.seen[e]
        if s.get(c, 0) >= v:
            return
        s[c] = v
        self.prog[e].append(("w", c.h, v))

    def _deps(self, e, reads, writes, strict=False):
        me = None if strict else self.ctr[e]
        for b in reads:
            if b.w is not None and not (b.w[0] is me and e == "pe"):
                self._wait(e, b.w)
        for b in writes:
            if b.w is not None and not (b.w[0] is me and e == "pe"):
                self._wait(e, b.w)
            for c, v in b.r.items():
                if c is me:
                    continue
                self._wait(e, (c, v))

    def _mark(self, tk, reads, writes):
        for b in reads:
            if b.r.get(tk[0], 0) < tk[1]:
                b.r[tk[0]] = tk[1]
        for b in writes:
            b.w = tk
            b.r = {}

    def op(self, e, fn, reads=(), writes=(), inc=True):
        if self.dead:
            return None
        self._deps(e, reads, writes)
        c = self.ctr[e]
        if inc:
            c.n += 1
            tk = (c, c.n)
            self.prog[e].append(("o", fn, c.h, 1))
        else:
            tk = (c, c.n + 1)
            self.prog[e].append(("o", fn, None, 0))
        self._mark(tk, reads, writes)
        return tk

    def dma(self, q, ctr, fn, reads=(), writes=()):
        if self.dead:
            return None
        self._deps(q, reads, writes, strict=True)
        ctr.n += 16
        tk = (ctr, ctr.n)
        self.prog[q].append(("o", fn, ctr.h, 16))
        self._mark(tk, reads, writes)
        return tk

    def cc(self, ctr, fn, reads=(), writes=()):
        if self.dead:
            return None
        self._deps("pool", reads, writes)
        ctr.n += 1
        tk = (ctr, ctr.n)
        self.prog["pool"].append(("o", fn, ctr.h, 1))
        self._mark(tk, reads, writes)
        return tk

    def barrier(self):
        if self.dead:
            return
        for e in self.ENG:
            for c in self.allctr:
                if c.n > 0 and c is not self.ctr[e]:
                    self._wait(e, (c, c.n))

    def emit(self):
        nc = self.nc
        with nc.Block() as block:
            def run(name):
                def body(eng):
                    for it in self.prog[name]:
                        if it[0] == "w":
                            eng.wait_ge(it[1], it[2])
                        else:
                            ins = it[1](eng)
                            if it[2] is not None:
                                ins.then_inc(it[2], it[3])
                return body
            block.sync(run("sp"))
            block.tensor(run("pe"))
            block.scalar(run("act"))
            block.vector(run("dve"))
            block.gpsimd(run("pool"))


class _Stop(Exception):
    pass


def build_program(TL, dbg=None, stop=None):
    NB = TL // 128
    NCB = CTXL // 128
    nc = bass.Bass("TRN2", target_bir_lowering=False)

    def din(name, shape):
        return nc.dram_tensor(name, list(shape), F32, kind="ExternalInput")

    x_d = din("x", [TL, D])
    xh_d = din("xh", [128, D])
    ctx_d = din("ctx", [CTXL, D])
    wmod_d = din("wmod", [D, 1536])
    bmod_d = din("bmod", [128, 12])
    cvec_d = din("cvec", [128, KD, 2])
    n1g_d = din("n1g", [128, KD])
    n2g_d = din("n2g", [128, KD])
    fg_d = din("fg", [1, D])
    win_d = din("w_in", [D, INC])
    lb_d = din("lbraw", [1, 2 * 2 * AW])
    ang_d = din("ang", [1, 128])
    sw_d = din("sconvw", [128, 8, 3])
    wpa_d = din("w_pa", [AW, D])
    wpb_d = din("w_pb", [AW, D])
    wo_d = din("w_o", [D, D])
    wup_d = din("w_up", [D, 2 * DFF])
    dw_d = din("ffndw", [128, NFC, 9])
    db_d = din("ffndb", [128, NFC])
    wdn_d = din("w_down", [DFF, D])
    cst_d = din("consts", [128, 7, 128])
    cinfo_d = din("cinfo", [128, 48])
    cmask_d = din("cmask", [128, 2, 1200])
    out_d = nc.dram_tensor("out", [TL, D], F32, kind="ExternalOutput")
    dbg_d = {}
    if dbg:
        for nm, shp in dbg.items():
            dbg_d[nm] = nc.dram_tensor("dbg_" + nm, list(shp), F32, kind="ExternalOutput")

    modin_d = nc.dram_tensor("modin", [128, 24], F32)
    modout_d = nc.dram_tensor("modout", [128 * NCORES, 24], F32)
    stin_d = [nc.dram_tensor(f"stin{h}", [128, 258], F32) for h in range(NH)]
    stout_d = [nc.dram_tensor(f"stout{h}", [128 * NCORES, 258], F32) for h in range(NH)]
    xmid_d = nc.dram_tensor("xmid", [TL, D], F32)
    ya_d = nc.dram_tensor("ya_scr", [128, NH, TL], BF16)
    hbin_d = nc.dram_tensor("hbin", [128, D], F32)
    hbout_d = nc.dram_tensor("hbout", [128 * NCORES, D], F32)

    def bcast(h, n, off=0):
        return bass.AP(h, off, [[0, 128], [1, n]])

    with ExitStack() as es:
        k = K(nc, es)
        c_const = k.newctr("ld_const")
        c_cc = k.newctr("cc")
        c_misc = k.newctr("ld_misc")
        c_st = k.newctr("st")

        if dbg:
            dbgbuf = k.sb("dbgbuf", [128, 2048], F32)
            B_dbg = Buf("dbg")
        cst = k.sb("cst", [128, 7, 128], F32)
        B_cst = Buf("cst")
        ident = cst[:, 0, :]
        U = [cst[:, 1, :], cst[:, 2, :]]
        Wm = [cst[:, 3, :], cst[:, 4, :]]
        ones = cst[:, 5, :]
        cinfo = k.sb("cinfo", [128, 48], F32)
        B_ci = Buf("cinfo")
        modv = k.sb("modv", [128, 96, 2], F32)
        B_mod = Buf("modv")
        gm = k.sb("gm", [128, 3, KD], F32)
        B_gm = Buf("gm")
        n1g = k.sb("n1g", [128, KD], F32)
        n2g = k.sb("n2g", [128, KD], F32)
        swt = k.sb("swt", [128, 8, 3], F32)
        dwt = k.sb("dwt", [128, NFC, 9], F32)
        dbt = k.sb("dbt", [128, NFC], F32)
        ang = k.sb("angbc", [128, 128], F32)
        epsc = k.sb("epsc", [128, 1], F32)
        B_small = Buf("small")

        for (t, src) in ((cst[:], cst_d.ap()), (cinfo[:], cinfo_d.ap()), (n1g[:], n1g_d.ap()), (n2g[:], n2g_d.ap()),
                         (swt[:], sw_d.ap()), (dwt[:], dw_d.ap()), (dbt[:], db_d.ap()), (ang[:], bcast(ang_d, 128))):
            k.dma("sp", c_const, lambda e, t=t, src=src: e.dma_start(out=t, in_=src), writes=[B_cst, B_ci, B_small])

        psr = [(k.ps(f"ps{i}", [128, 512], F32), Buf(f"ps{i}")) for i in range(8)]
        psi = [0]

        def psum():
            i = psi[0]
            psi[0] = (i + 1) % 8
            return psr[i]

        STG = 1024
        stg = []
        stgi = [0]
        STGSZ = [1024]
        casti = [0]
        stg_ctrs = [k.newctr(f"ldw{i}") for i in range(4)]

        def set_stg(scope, n, size):
            del stg[:]
            for i in range(n):
                stg.append((k.sb(f"stg{i}", [128, size], F32, scope), Buf(f"stg{i}"), stg_ctrs[i]))
            stgi[0] = 0
            STGSZ[0] = size

        def fetch(dst, dstbuf, src, kc, ncol, cast_eng=None):
            if kc * ncol > STGSZ[0]:
                h_ = kc // 2
                fetch(dst[:, 0:h_, :], dstbuf, src[:, 0:h_, :], h_, ncol, cast_eng)
                fetch(dst[:, h_:kc, :], dstbuf, src[:, h_:kc, :], kc - h_, ncol, cast_eng)
                return
            t, b, c = stg[stgi[0]]
            stgi[0] = (stgi[0] + 1) % len(stg)
            sv = t[:, 0:kc * ncol].rearrange("p (c n) -> p c n", c=kc)
            k.dma("sp", c, lambda e: e.dma_start(out=sv, in_=src), writes=[b])
            if cast_eng is None:
                cast_eng = ("dve", "act")[casti[0] % 2]
                casti[0] += 1
            cp(cast_eng, dst, sv, [b], [dstbuf])

        def wsrc(wd, r0c, kc, c0, ncol):
            return wd.ap()[r0c * 128:(r0c + kc) * 128, c0:c0 + ncol].rearrange("(c p) n -> p c n", p=128)

        def mk_wring(scope, cast_eng=None):
            NWR = 4
            wring = [(k.sb(f"wr{i}", [128, STG], BF16, scope), Buf(f"wr{i}")) for i in range(NWR)]
            wri = [0]

            def wtile(wd, r0c, kc, c0, ncol):
                t, b = wring[wri[0]]
                wri[0] = (wri[0] + 1) % NWR
                v = t[:, 0:kc * ncol].rearrange("p (c n) -> p c n", c=kc)
                fetch(v, b, wsrc(wd, r0c, kc, c0, ncol), kc, ncol, cast_eng)
                return v, b
            return wtile

        def act(out, in_, func, reads, writes, **kw):
            return k.op("act", lambda e: e.activation(out=out, in_=in_, func=func, **kw), reads, writes)

        def tt(eng, out, a, b, op, reads, writes):
            return k.op(eng, lambda e: e.tensor_tensor(out=out, in0=a, in1=b, op=op), reads, writes)

        def ts(eng, out, a, s1, s2, op0, op1, reads, writes):
            if s2 is None:
                return k.op(eng, lambda e: e.tensor_scalar(out=out, in0=a, scalar1=s1, scalar2=None, op0=op0), reads, writes)
            return k.op(eng, lambda e: e.tensor_scalar(out=out, in0=a, scalar1=s1, scalar2=s2, op0=op0, op1=op1), reads, writes)

        def stt(eng, out, a, s, b, op0, op1, reads, writes):
            return k.op(eng, lambda e: e.scalar_tensor_tensor(out=out, in0=a, scalar=s, in1=b, op0=op0, op1=op1), reads, writes)

        def cp(eng, out, in_, reads, writes):
            if eng == "act":
                return k.op("act", lambda e: e.copy(out=out, in_=in_), reads, writes)
            return k.op(eng, lambda e: e.tensor_copy(out=out, in_=in_), reads, writes)

        def mm(out, lhsT, rhs, start, stop, reads, writes, inc=None):
            return k.op("pe", lambda e: e.matmul(out, lhsT=lhsT, rhs=rhs, start=start, stop=stop), reads, writes,
                        inc=(True if inc is None else inc))

        def tr(out, in_, reads, writes, inc=True):
            return k.op("pe", lambda e: e.transpose(out=out, in_=in_, identity=ident), list(reads) + [B_cst], writes, inc=inc)

        def rstd_from(ssq, n, tmpbuf):
            act(ssq, ssq, AF.Ln, [tmpbuf, B_small], [tmpbuf], scale=1.0 / n, bias=epsc[:])
            act(ssq, ssq, AF.Exp, [tmpbuf], [tmpbuf], scale=-0.5)

        def dump(name, ap, buf):
            if name in dbg_d:
                k.dma("sp", c_st, lambda e: e.dma_start(out=dbg_d[name].ap(), in_=ap), reads=[buf])

        try:
            with ExitStack() as s0:
                cv = k.sb("cv", [128, KD, 2], F32, s0)
                B_cv = Buf("cv")
                k.dma("sp", c_misc, lambda e: e.dma_start(out=cv[:], in_=cvec_d.ap()), writes=[B_cv])
                act(cv[:], cv[:], AF.Silu, [B_cv], [B_cv])
                bm = k.sb("bm", [128, 12], F32, s0)
                k.dma("sp", c_misc, lambda e: e.dma_start(out=bm[:], in_=bmod_d.ap()), writes=[B_cv])
                wm = k.sb("wmf", [128, KD, 512], F32, s0)
                B_wm = Buf("wm")
                mloc = k.sb("mloc", [128, 12, 2], F32, s0)
                B_ml = Buf("mloc")
                for g3 in range(3):
                    k.dma("sp", c_misc, lambda e, g3=g3: e.dma_start(
                        out=wm[:], in_=wmod_d.ap()[:, g3 * 512:(g3 + 1) * 512].rearrange("(c p) n -> p c n", p=128)),
                        writes=[B_wm])
                    for j4 in range(4):
                        j = g3 * 4 + j4
                        pt, pb = psum()
                        for c in range(KD):
                            mm(pt[:, 0:2], wm[:, c, j4 * 128:(j4 + 1) * 128], cv[:, c, :], c == 0, c == KD - 1,
                               [B_wm, B_cv], [pb])
                        ts("dve", mloc[:, j, :], pt[:, 0:2], bm[:, j:j + 1], None, ALU.add, None, [pb, B_cv], [B_ml])
                k.dma("sp", c_misc, lambda e: e.dma_start(out=modin_d.ap(), in_=mloc[:].rearrange("p j t -> p (j t)")),
                      reads=[B_ml], writes=[B_ml])
                B_mo = Buf("modout")
                k.cc(c_cc, lambda e: e.collective_compute("AllGather", ALU.bypass, replica_groups=[list(range(NCORES))],
                                                          ins=[modin_d.ap().opt()], outs=[modout_d.ap().opt()]),
                     reads=[B_ml], writes=[B_mo])
                k.dma("sp", c_misc, lambda e: e.dma_start(
                    out=modv[:].rearrange("p (r j) t -> p r (j t)", r=NCORES),
                    in_=modout_d.ap().rearrange("(r p) f -> p r f", p=128)), reads=[B_mo], writes=[B_mod])
                stt("dve", gm[:, 0, :], modv[:, 16:32, 0], 1.0, n1g[:], ALU.add, ALU.mult, [B_mod, B_small], [B_gm])
                stt("dve", gm[:, 1, :], modv[:, 16:32, 1], 1.0, n1g[:], ALU.add, ALU.mult, [B_mod, B_small], [B_gm])
                stt("dve", gm[:, 2, :], modv[:, 64:80, 0], 1.0, n2g[:], ALU.add, ALU.mult, [B_mod, B_small], [B_gm])
                k.op("dve", lambda e: e.memset(epsc[:], EPS), [], [B_small])
                k.barrier()
                if stop == "S0":
                    k.dead = True
            sh1 = lambda c: modv[:, 0 + c, 0:1]
            shc = lambda c: modv[:, 0 + c, 1:2]
            sh2 = lambda c: modv[:, 48 + c, 0:1]

            def mk_norm(scope, nx):
                xring = [(k.sb(f"xr{i}", [128, D], F32, scope), Buf(f"xr{i}"), k.newctr(f"ldx{len(k.allctr)}")) for i in range(nx)]
                xri = [0]
                xnS = k.sb("xnS", [128, D], F32, scope)
                B_xn = Buf("xnS")
                nss = k.sb("nrmss", [128, 1], F32, scope)
                B_nss = Buf("nss")

                def xload(src_ap, rows=None):
                    t, b, c = xring[xri[0]]
                    xri[0] = (xri[0] + 1) % nx
                    dst = t[:] if rows is None else t[rows[0]:rows[1], :]
                    k.dma("sp", c, lambda e: e.dma_start(out=dst, in_=src_ap), writes=[b])
                    return t, b

                def norm_T(xt, xb, gsel, shf, dstT, dstbuf, pieces):
                    act(xnS[:], xt[:], AF.Square, [xb], [B_xn, B_nss], accum_out=nss[:])
                    rstd_from(nss[:], D, B_nss)
                    act(xnS[:], xt[:], AF.Copy, [xb, B_nss], [B_xn], scale=nss[:])
                    for c4 in range(4):
                        pt, pb = psum()
                        for i in range(4):
                            c = c4 * 4 + i
                            tr(pt[:, i * 128:(i + 1) * 128], xnS[:, c * 128:(c + 1) * 128], [B_xn], [pb], inc=(i == 3))
                        for i in range(4):
                            c = c4 * 4 + i
                            for (col0, ncols, pcol0) in pieces:
                                act(dstT[:, c, col0:col0 + ncols], pt[:, i * 128 + pcol0:i * 128 + pcol0 + ncols], AF.Identity,
                                    [pb, B_gm, B_mod], [dstbuf], scale=gm[:, gsel, c:c + 1], bias=shf(c))
                return xload, norm_T, xring, (xnS, B_xn)

            B_ya = Buf("ya_scr")
            c_ya = k.newctr("st_ya")
            sHX = ExitStack()
            es.enter_context(sHX)
            hxT = k.sb("hxT", [128, KD, TL], BF16, sHX)
            B_hx = [Buf(f"hx{i}") for i in range(NB)]
            hcT = k.sb("hcT", [128, KD, CTXL], BF16, sHX)
            B_hc = [Buf(f"hc{i}") for i in range(NCB)]
            with ExitStack() as s1:
                xload, norm_T, _, _ = mk_norm(s1, 2)
                for tb in range(NB):
                    xt, xb = xload(x_d.ap()[tb * 128:(tb + 1) * 128, :])
                    norm_T(xt, xb, 0, sh1, hxT, B_hx[tb], [(tb * 128, 128, 0)])
                for tb in range(NCB):
                    xt, xb = xload(ctx_d.ap()[tb * 128:(tb + 1) * 128, :])
                    norm_T(xt, xb, 1, shc, hcT, B_hc[tb], [(tb * 128, 128, 0)])
                k.barrier()
                if stop == "S1":
                    k.dead = True
            with ExitStack() as s2:
                set_stg(s2, 3, 1024)
                G = min(4, NB)
                NCH = 2 * NB
                ctxS_t = [k.sb(f"ctxS{d}", [128, 128], F32, s2) for d in range(2)]
                whs = [(k.sb(f"wh{i}", [128, KD, 384], BF16, s2), Buf(f"wh{i}")) for i in range(2)]
                wh, B_wh = whs[0]
                whq = k.sb("whq", [128, KD, 128], BF16, s2)
                B_whq = Buf("whq")

                def fetch_wh(h_, d_):
                    wv_, wb__ = whs[(2 * h_ + d_) % 2]
                    if d_ == 0:
                        cols_ = (AW + h_ * 128, 3 * AW + h_ * 128, 4 * AW + h_ * 128)
                    else:
                        cols_ = (2 * AW + h_ * 128, 3 * AW + h_ * 128)
                    for pi, c0 in enumerate(cols_):
                        fetch(wv_[:, :, pi * 128:(pi + 1) * 128], wb__, wsrc(win_d, 0, KD, c0, 128), KD, 128)

                def fetch_q(h_):
                    fetch(whq[:], B_whq, wsrc(win_d, 0, KD, h_ * 128, 128), KD, 128)
                qsT = k.sb("qsT", [128, TL], BF16, s2)
                B_qs = Buf("qsT")
                qdT = k.sb("qdT", [128, 2, TL], BF16, s2)
                B_qd = [Buf("qd0"), Buf("qd1")]
                opart = k.sb("opart", [128, NB, 128], F32, s2)
                B_op = Buf("opart")
                ggS = k.sb("ggS", [128, NB, 128], BF16, s2)
                B_gg = Buf("gg")
                KVs = k.sb("KVs", [128, NCH, 128], F32, s2)
                B_KV = Buf("KVs")
                Sbf = k.sb("Sbf", [128, NCH, 128], BF16, s2)
                B_Sbf = Buf("Sbf")
                Zs = k.sb("Zs", [128, 128], BF16, s2)
                Acol = k.sb("Acol", [128, NCH], F32, s2)
                B_A = Buf("Acol")
                PcJ = k.sb("PcJ", [128, NCH], F32, s2)
                Pend = k.sb("Pend", [128, 1], F32, s2)
                B_Pc = Buf("Pc")
                lbt = k.sb("lbt", [128, 2, 2, 128], F32, s2)
                lbv = lbt
                B_lb = Buf("lb")
                summ = k.sb("summ", [128, 2, 129], F32, s2)
                B_sum = Buf("summ")
                agT = k.sb("agT", [128, NCORES, 129], F32, s2)
                B_ag = Buf("agT")
                Sin = k.sb("Sin", [128, 2, 128], BF16, s2)
                B_Sin = Buf("Sin")
                pm8 = k.sb("pm8", [128, NCORES], F32, s2)
                B_pm8 = Buf("pm8")
                tmpS = k.sb("tmpS", [128, 128], F32, s2)
                B_tmp = Buf("tmpS")
                wkt = {}
                for nm in ("fG", "lgG", "kSG", "ebG", "enbG", "eeG", "tmpG"):
                    wkt[nm] = (k.sb(f"wk_{nm}", [128, G, 128], F32, s2), Buf(nm))
                for nm in ("vSG", "kdG", "scmG"):
                    wkt[nm] = (k.sb(f"wk_{nm}", [128, G, 128], BF16, s2), Buf(nm))
                kdzG = k.sb("wk_kdz", [128, G, 2, 128], BF16, s2)
                B_kdz = Buf("kdz")
                ssG = k.sb("wk_ss", [128, G], F32, s2)
                B_ss = Buf("ssG")
                k.op("pool", lambda e: e.memset(kdzG[:], 0.0), [], [B_kdz])
                ystg = [(k.sb("ystg0", [128, G * 128], BF16, s2), Buf("ystg0"))] * 2
                ysi = [0]
                k.op("pool", lambda e: e.memset(Zs[:], 0.0), [], [B_Sbf])
                fG, B_f = wkt["fG"]
                lgG, B_lg = wkt["lgG"]
                kSG, B_kS = wkt["kSG"]
                lgGs = [wkt["lgG"], (k.sb("wk_lgG2", [128, G, 128], F32, s2), Buf("lgG2"))]
                kSGs = [wkt["kSG"], (k.sb("wk_kSG2", [128, G, 128], F32, s2), Buf("kSG2"))]
                vSGs = [wkt["vSG"], (k.sb("wk_vSG2", [128, G, 128], BF16, s2), Buf("vSG2"))]
                ebG, B_eb = wkt["ebG"]
                enbG, B_enb = wkt["enbG"]
                eeG, B_ee = wkt["eeG"]
                tmpG, B_tg = wkt["tmpG"]
                vSG, B_v = wkt["vSG"]
                kdG, B_kd = wkt["kdG"]
                scmG, B_scm = wkt["scmG"]

                def bc3(ap2, g):
                    return ap2.unsqueeze(1).to_broadcast([128, g, 128])

                def sig_inplace(ap, buf):
                    act(ap, ap, AF.Ln, [buf, B_cst], [buf], bias=ones[:, 0:1])
                    act(ap, ap, AF.Exp, [buf], [buf], scale=-1.0)

                def sweep(h, d, actT_, B_act, nblk, is_ctx):
                    nz = 384 if (d == 0 and not is_ctx) else 256
                    gate = (d == 0 and not is_ctx)
                    Gs = min(G, nblk)
                    nch = 2 * nblk
                    KV4 = KVs[:, 0:nch, :].rearrange("p (b c) f -> p b c f", c=2)
                    A3 = Acol[:, 0:nch].rearrange("p (b c) -> p b c", c=2)
                    GW = Gs * 128

                    def a1(gi):
                        b0 = gi * Gs
                        lgG, B_lg = lgGs[gi % 2]
                        kSG, B_kS = kSGs[gi % 2]
                        vSG, B_v = vSGs[gi % 2]
                        Z = []
                        for i in range(Gs):
                            tb = b0 + i
                            zt, zb = psum()
                            for c in range(KD):
                                mm(zt[:, 0:nz], actT_[:, c, tb * 128:(tb + 1) * 128], wh[:, c, 0:nz], c == 0, c == KD - 1,
                                   [B_act[tb], B_wh], [zb], inc=(c == KD - 1))
                            Z.append((zt, zb))
                        for i, (zt, zb) in enumerate(Z):
                            act(fG[:, i, :], zt[:, 0:128], AF.Exp, [zb], [B_f], scale=-1.0)
                            cp("act", vSG[:, i, :], zt[:, 128:256], [zb], [B_v])
                            if gate:
                                act(tmpG[:, i, :], zt[:, 256:384], AF.Exp, [zb], [B_tg], scale=-1.0)
                        fv = fG[:, 0:Gs, :]
                        sig_inplace(fv, B_f)
                        tt("dve", fv, fv, bc3(lbv[:, 1, d, :], Gs), ALU.mult, [B_f, B_lb], [B_f])
                        tt("dve", fv, fv, bc3(lbv[:, 0, d, :], Gs), ALU.add, [B_f, B_lb], [B_f])
                        act(lgG[:, 0:Gs, :], fv, AF.Ln, [B_f], [B_lg])
                        ts("dve", kSG[:, 0:Gs, :], fv, -1.0, 1.0, ALU.mult, ALU.add, [B_f], [B_kS])
                        if gate:
                            sig_inplace(tmpG[:, 0:Gs, :], B_tg)
                            for i, (zt, zb) in enumerate(Z):
                                tt("dve", tmpG[:, i, :], tmpG[:, i, :], zt[:, 256:384], ALU.mult, [B_tg, zb], [B_tg])
                            tt("pool", ggS[:, b0:b0 + Gs, :], tmpG[:, 0:Gs, :], bc3(ang[:], Gs), ALU.mult, [B_tg, B_small], [B_gg])

                    def a2(gi):
                        b0 = gi * Gs
                        tok0 = b0 * 128
                        lgG, B_lg = lgGs[gi % 2]
                        kSG, B_kS = kSGs[gi % 2]
                        vSG, B_v = vSGs[gi % 2]
                        Pb, bPb = psum()
                        Pe, bPe = psum()
                        Pk, bPk = psum()
                        for i in range(Gs):
                            cs = slice(i * 128, (i + 1) * 128)
                            mm(Pb[:, cs], lgG[:, i, :], U[d], True, True, [B_lg, B_cst], [bPb], inc=(i == Gs - 1))
                        for i in range(Gs):
                            cs = slice(i * 128, (i + 1) * 128)
                            mm(Pe[:, cs], Wm[d], lgG[:, i, :], True, True, [B_lg, B_cst], [bPe], inc=(i == Gs - 1))
                        if not is_ctx:
                            for i in range(Gs):
                                cs = slice(i * 128, (i + 1) * 128)
                                tr(Pk[:, cs], kSG[:, i, :], [B_kS], [bPk], inc=(i == Gs - 1))
                        Pb3 = Pb[:, 0:GW].rearrange("p (g f) -> p g f", g=Gs)
                        Pe3 = Pe[:, 0:GW].rearrange("p (g f) -> p g f", g=Gs)
                        Pk3 = Pk[:, 0:GW].rearrange("p (g f) -> p g f", g=Gs)
                        act(ebG[:, 0:Gs, :], Pb3, AF.Exp, [bPb], [B_eb])
                        act(eeG[:, 0:Gs, :], Pe3, AF.Exp, [bPe], [B_ee])
                        tt("pool", kdzG[0:64, 0:Gs, 0, :], kSG[0:64, 0:Gs, :], eeG[0:64, 0:Gs, :], ALU.mult, [B_kS, B_ee], [B_kdz])
                        tt("pool", kdzG[64:128, 0:Gs, 1, :], kSG[64:128, 0:Gs, :], eeG[64:128, 0:Gs, :], ALU.mult, [B_kS, B_ee], [B_kdz])
                        for cidx in range(2):
                            acol = (cidx * 64 + 63) if d == 0 else cidx * 64
                            cp("pool", A3[:, b0:b0 + Gs, cidx], ebG[:, 0:Gs, acol], [B_eb], [B_A])
                        if not is_ctx:
                            act(enbG[:, 0:Gs, :], Pb3, AF.Exp, [bPb], [B_enb], scale=-1.0)
                            qd3 = qdT[:, d, tok0:tok0 + GW].rearrange("p (g f) -> p g f", g=Gs)
                            qs3 = qsT[:, tok0:tok0 + GW].rearrange("p (g f) -> p g f", g=Gs)
                            tt("dve", qd3, qs3, ebG[:, 0:Gs, :], ALU.mult, [B_qs, B_eb], [B_qd[d]])
                            tt("dve", kdG[:, 0:Gs, :], Pk3, enbG[:, 0:Gs, :], ALU.mult, [bPk, B_enb], [B_kd])
                            Ps, bPs = psum()
                            for i in range(Gs):
                                tb = b0 + i
                                mm(Ps[:, i * 128:(i + 1) * 128], kdG[:, i, :], qdT[:, d, tb * 128:(tb + 1) * 128], True, True,
                                   [B_kd, B_qd[d]], [bPs], inc=(i == Gs - 1))
                            Ps3 = Ps[:, 0:GW].rearrange("p (g f) -> p g f", g=Gs)
                            tt("dve", scmG[:, 0:Gs, :], Ps3, bc3(U[d], Gs), ALU.mult, [bPs, B_cst], [B_scm])
                            Po, bPo = psum()
                            for i in range(Gs):
                                mm(Po[:, i * 128:(i + 1) * 128], scmG[:, i, :], vSG[:, i, :], True, True, [B_scm, B_v], [bPo],
                                   inc=(i == Gs - 1))
                            Po3 = Po[:, 0:GW].rearrange("p (g f) -> p g f", g=Gs)
                            if d == 0:
                                cp("act", opart[:, b0:b0 + Gs, :], Po3, [bPo], [B_op])
                            else:
                                tt("dve", opart[:, b0:b0 + Gs, :], opart[:, b0:b0 + Gs, :], Po3, ALU.add, [B_op, bPo], [B_op])
                        PkA, bPkA = psum()
                        PkB, bPkB = psum()
                        for (cidx, pk_, bpk_) in ((0, PkA, bPkA), (1, PkB, bPkB)):
                            for i in range(Gs):
                                mm(pk_[:, i * 128:(i + 1) * 128], kdzG[:, i, cidx, :], vSG[:, i, :], True, True, [B_kdz, B_v], [bpk_],
                                   inc=(i == Gs - 1))
                            cp("act", KV4[:, b0:b0 + Gs, cidx, :], pk_[:, 0:GW].rearrange("p (g f) -> p g f", g=Gs), [bpk_], [B_KV])
                    ngr_ = nblk // Gs
                    a1(0)
                    for gi in range(ngr_):
                        if gi + 1 < ngr_:
                            a1(gi + 1)
                        a2(gi)
                    order = list(range(nch)) if d == 0 else list(range(nch - 1, -1, -1))
                    if not is_ctx:
                        k.op("dve", lambda e: e.memset(PcJ[:, order[0]:order[0] + 1], 1.0), [], [B_Pc])
                    for p_, j in enumerate(order):
                        if p_ > 0:
                            jp = order[p_ - 1]
                            stt("dve", KVs[:, j, :], KVs[:, jp, :], Acol[:, j:j + 1], KVs[:, j, :], ALU.mult, ALU.add,
                                [B_KV, B_A], [B_KV])
                        if not is_ctx:
                            dst = PcJ[:, order[p_ + 1]:order[p_ + 1] + 1] if p_ + 1 < nch else Pend[:]
                            ts("dve", dst, PcJ[:, j:j + 1], Acol[:, j:j + 1], None, ALU.mult, None, [B_Pc, B_A], [B_Pc])
                    if is_ctx:
                        return order[-1]
                    for q4 in range(0, nch, 8):
                        n4 = min(8, nch - q4)
                        cp("act", Sbf[:, q4:q4 + n4, :], KVs[:, q4:q4 + n4, :], [B_KV], [B_Sbf])
                    for gi in range(nblk // Gs):
                        b0 = gi * Gs
                        GW = Gs * 128
                        PcA, bA = psum()
                        PcB, bB = psum()
                        for (cidx, pc_, bpc_) in ((0, PcA, bA), (1, PcB, bB)):
                            for i in range(Gs):
                                tb = b0 + i
                                j = 2 * tb + cidx
                                sidx = (j - 1) if d == 0 else (j + 1)
                                st_ = Sbf[:, sidx, :] if 0 <= sidx < nch else Zs[:]
                                mm(pc_[:, i * 128:(i + 1) * 128], qdT[:, d, tb * 128:(tb + 1) * 128], st_, True, True,
                                   [B_qd[d], B_Sbf], [bpc_], inc=(i == Gs - 1))
                        tt("dve", opart[0:64, b0:b0 + Gs, :], opart[0:64, b0:b0 + Gs, :],
                           PcA[0:64, 0:GW].rearrange("p (g f) -> p g f", g=Gs), ALU.add, [B_op, bA], [B_op])
                        tt("dve", opart[64:128, b0:b0 + Gs, :], opart[64:128, b0:b0 + Gs, :],
                           PcB[64:128, 0:GW].rearrange("p (g f) -> p g f", g=Gs), ALU.add, [B_op, bB], [B_op])
                    qd4 = qdT[:, d, :].rearrange("p (j t) -> p j t", t=64)
                    tt("pool", qd4, qd4, PcJ[:, 0:nch].unsqueeze(2).to_broadcast([128, nch, 64]), ALU.mult, [B_qd[d], B_Pc], [B_qd[d]])
                    return order[-1]

                fetch_q(0)
                fetch_wh(0, 0)
                for h in range(NH):
                    for sl in range(2):
                        for dd in range(2):
                            k.dma("sp", c_misc, lambda e, sl=sl, dd=dd: e.dma_start(
                                out=lbt[:, sl, dd, :], in_=bcast(lb_d, 128, (sl * 2 + dd) * AW + h * 128)), writes=[B_lb])
                    tt("dve", lbv[:, 0, :, :], lbt[:, 1, :, :], lbt[:, 0, :, :], ALU.subtract, [B_lb], [B_lb])
                    act(lbv[:, 0, :, :], lbv[:, 0, :, :], AF.Exp, [B_lb], [B_lb])
                    sig_inplace(lbv[:, 0, :, :], B_lb)
                    ts("dve", lbv[:, 1, :, :], lbv[:, 0, :, :], -1.0, 1.0, ALU.mult, ALU.add, [B_lb], [B_lb])
                    for d in range(2):
                        wh, B_wh = whs[(2 * h + d) % 2]
                        if d == 0:
                            fetch_wh(h, 1)
                        elif h + 1 < NH:
                            fetch_wh(h + 1, 0)
                        if d == 0:
                            TGq = min(512, TL)
                            etmp = ebG[:].rearrange("p g f -> p (g f)")
                            for tg in range(TL // TGq):
                                pq, bq = psum()
                                for c in range(KD):
                                    mm(pq[:, 0:TGq], whq[:, c, :], hxT[:, c, tg * TGq:(tg + 1) * TGq], c == 0, c == KD - 1,
                                       [B_whq] + B_hx, [bq], inc=(c == KD - 1))
                                act(etmp[:, 0:TGq], pq[:, 0:TGq], AF.Exp, [bq], [B_eb], scale=-1.0)
                                sig_inplace(etmp[:, 0:TGq], B_eb)
                                tt("dve", qsT[:, tg * TGq:(tg + 1) * TGq], pq[:, 0:TGq], etmp[:, 0:TGq], ALU.mult, [bq, B_eb], [B_qs])
                            if h + 1 < NH:
                                fetch_q(h + 1)
                        jl = sweep(h, d, hcT, B_hc, NCB, True)
                        cp("pool", ctxS_t[d][:], KVs[:, jl, :], [B_KV], [B_tmp])
                        jl = sweep(h, d, hxT, B_hx, NB, False)
                        cp("pool", summ[:, d, 0:128], KVs[:, jl, :], [B_KV], [B_sum])
                        cp("pool", summ[:, d, 128:129], Pend[:], [B_Pc], [B_sum])
                    B_si = Buf("stin")
                    B_so = Buf("stout")
                    k.dma("sp", c_misc, lambda e, h=h: e.dma_start(out=stin_d[h].ap(), in_=summ[:].rearrange("p d f -> p (d f)")),
                          reads=[B_sum], writes=[B_si])
                    k.cc(c_cc, lambda e, h=h: e.collective_compute(
                        "AllGather", ALU.bypass, replica_groups=[list(range(NCORES))],
                        ins=[stin_d[h].ap().opt()], outs=[stout_d[h].ap().opt()]), reads=[B_si], writes=[B_so])
                    for d in range(2):
                        k.dma("sp", c_misc, lambda e, h=h, d=d: e.dma_start(
                            out=agT[:], in_=stout_d[h].ap()[:, d * 129:(d + 1) * 129].rearrange("(r p) f -> p r f", p=128)),
                            reads=[B_so], writes=[B_ag])
                        acc = ctxS_t[d]
                        rr = range(NCORES) if d == 0 else range(NCORES - 1, -1, -1)
                        tt("dve", pm8[:], agT[:, :, 128], cinfo[:, d * 8:d * 8 + 8], ALU.mult, [B_ag, B_ci], [B_pm8])
                        tt("dve", pm8[:], pm8[:], cinfo[:, 16 + d * 8:16 + d * 8 + 8], ALU.add, [B_pm8, B_ci], [B_pm8])
                        tt("pool", agT[:, :, 0:128], agT[:, :, 0:128],
                           cinfo[:, d * 8:d * 8 + 8].unsqueeze(2).to_broadcast([128, NCORES, 128]), ALU.mult, [B_ag, B_ci], [B_ag])
                        for r in rr:
                            stt("dve", acc[:], acc[:], pm8[:, r:r + 1], agT[:, r, 0:128], ALU.mult, ALU.add, [B_tmp, B_pm8, B_ag], [B_tmp])
                        cp("act", Sin[:, d, :], acc[:], [B_tmp], [B_Sin])
                    oSG, B_oS = fG, B_f
                    yaG, B_yaG = lgG, B_lg
                    for gi in range(NB // G):
                        b0 = gi * G
                        GW = G * 128
                        Pp, bPp = psum()
                        for i in range(G):
                            tb = b0 + i
                            mm(Pp[:, i * 128:(i + 1) * 128], qdT[:, 0, tb * 128:(tb + 1) * 128], Sin[:, 0, :], True, False,
                               [B_qd[0], B_Sin], [bPp], inc=False)
                            mm(Pp[:, i * 128:(i + 1) * 128], qdT[:, 1, tb * 128:(tb + 1) * 128], Sin[:, 1, :], False, True,
                               [B_qd[1], B_Sin], [bPp], inc=(i == G - 1))
                        tt("dve", oSG[:], Pp[:, 0:GW].rearrange("p (g f) -> p g f", g=G), opart[:, b0:b0 + G, :], ALU.add,
                           [bPp, B_op], [B_oS])
                        for i in range(G):
                            act(yaG[:, i, :], oSG[:, i, :], AF.Square, [B_oS], [B_yaG, B_ss], accum_out=ssG[:, i:i + 1])
                        rstd_from(ssG[:], 128, B_ss)
                        for i in range(G):
                            stt("dve", yaG[:, i, :], oSG[:, i, :], ssG[:, i:i + 1], ggS[:, b0 + i, :], ALU.mult, ALU.mult,
                                [B_oS, B_ss, B_gg, B_yaG], [B_yaG])
                        Pt, bPt = psum()
                        for i in range(G):
                            tr(Pt[:, i * 128:(i + 1) * 128], yaG[:, i, :], [B_yaG], [bPt], inc=(i == G - 1))
                        yst, byst = ystg[ysi[0]]
                        ysi[0] = 1 - ysi[0]
                        cp("act", yst[:, 0:GW], Pt[:, 0:GW], [bPt], [byst])
                        k.dma("sp", c_ya, lambda e, yst=yst, h=h, b0=b0, GW=GW: e.dma_start(
                            out=ya_d.ap()[:, h, b0 * 128:b0 * 128 + GW], in_=yst[:, 0:GW]), reads=[byst], writes=[B_ya])
                k.barrier()
            if stop == "S2":
                k.dead = True
            sHX.close()

            TH = min(512, TL)
            NQ = TL // TH
            nbq = TH // 128
            B_xmid = [Buf(f"xmid{i}") for i in range(NB)]
            B_hbin = Buf("hbin")
            with ExitStack() as s3:
                set_stg(s3, 3, 1024)
                xload3, norm_T3, _, _ = mk_norm(s3, nbq)
                wtile = mk_wring(s3)
                g1bc = k.sb("g1bc", [128, D], F32, s3)
                B_gbc = Buf("gbc")
                dg = k.sb("dg", [128, 128], F32, s3)
                B_dg = Buf("dg")
                for c4 in range(4):
                    pt, pb = psum()
                    for i in range(4):
                        c = c4 * 4 + i
                        ts("dve", dg[:], ident, modv[:, 32 + c, 0:1], None, ALU.mult, None, [B_cst, B_mod], [B_dg])
                        mm(pt[:, i * 128:(i + 1) * 128], ones, dg[:], True, True, [B_cst, B_dg], [pb])
                    cp("act", g1bc[:, c4 * 512:(c4 + 1) * 512], pt[:], [pb], [B_gbc])
                hxq = k.sb("hxq", [128, KD, TH + 2], BF16, s3)
                B_hq = Buf("hxq")
                hal2 = k.sb("hal2", [128, D], F32, s3)
                B_hl2 = Buf("hal2")
                c_hl2 = k.newctr("ld_hal2")
                k.op("pool", lambda e: e.memset(hal2[:], 0.0), [], [B_hl2])
                yaq = k.sb("yaq", [128, NH, TH], BF16, s3)
                B_yaq = Buf("yaq")
                c_yaq = k.newctr("ld_yaq")
                ybT = k.sb("ybT", [128, 8, TH], BF16, s3)
                B_yb = Buf("ybT")
                mgT = k.sb("mgT", [128, KD, TH], BF16, s3)
                B_mg = Buf("mgT")
                ws3s = [(k.sb(f"ws3_{i}", [128, KD, 384], BF16, s3), Buf(f"ws3_{i}")) for i in range(2)]
                wp3s = [(k.sb(f"wp3_{i}", [128, 8, 256], BF16, s3), Buf(f"wp3_{i}")) for i in range(2)]
                jobs3 = [("sc", cc) for cc in range(8)] + [("mg", m) for m in range(KD)]

                def fetch_job(ji):
                    kind, i_ = jobs3[ji]
                    wsb, bws = ws3s[ji % 2]
                    if kind == "sc":
                        for pi, c0 in enumerate((5 * AW + i_ * 128, 6 * AW + i_ * 128, 7 * AW + i_ * 128)):
                            fetch(wsb[:, :, pi * 128:(pi + 1) * 128], bws, wsrc(win_d, 0, KD, c0, 128), KD, 128)
                    else:
                        for pi, c0 in enumerate((8 * AW + i_ * 128, 10 * AW + i_ * 128)):
                            fetch(wsb[:, :, pi * 128:(pi + 1) * 128], bws, wsrc(win_d, 0, KD, c0, 128), KD, 128)
                        wpb_t, bwp = wp3s[ji % 2]
                        fetch(wpb_t[:, :, 0:128], bwp, wsrc(wpa_d, 0, 8, i_ * 128, 128), 8, 128)
                        fetch(wpb_t[:, :, 128:256], bwp, wsrc(wpb_d, 0, 8, i_ * 128, 128), 8, 128)
                uext = k.sb("uext", [128, TH + 2], F32, s3)
                B_u = Buf("uext")
                scS = k.sb("scS", [128, 512], F32, s3)
                B_sc = Buf("scS")
                accS = k.sb("accS", [128, TH], F32, s3)
                B_acc = Buf("accS")
                sgS = [(k.sb(f"sgS{i}", [128, 512], F32, s3), Buf(f"sgS{i}")) for i in range(2)]
                t12 = [(k.sb(f"t12{i}", [128, 512], F32, s3), Buf(f"t12{i}")) for i in range(2)]
                tmpx = k.sb("tmpx", [128, 512], F32, s3)
                B_tx = Buf("tmpx")
                for q in range(NQ):
                    t0 = q * TH
                    xq = []
                    for tbl in range(nbq):
                        tbg = t0 // 128 + tbl
                        xt, xb = xload3(x_d.ap()[tbg * 128:(tbg + 1) * 128, :])
                        norm_T3(xt, xb, 0, sh1, hxq, B_hq, [(tbl * 128, 128, 0)])
                        xq.append((xt, xb))
                    k.dma("sp", c_yaq, lambda e, t0=t0: e.dma_start(out=yaq[:], in_=ya_d.ap()[:, :, t0:t0 + TH]),
                          reads=[B_ya], writes=[B_yaq])
                    lsrc = x_d.ap()[t0 - 1:t0, :] if t0 > 0 else xh_d.ap()[0:1, :]
                    rsrc = x_d.ap()[t0 + TH:t0 + TH + 1, :] if t0 + TH < TL else xh_d.ap()[1:2, :]
                    k.dma("sp", c_hl2, lambda e, lsrc=lsrc: e.dma_start(out=hal2[0:1, :], in_=lsrc), writes=[B_hl2])
                    k.dma("sp", c_hl2, lambda e, rsrc=rsrc: e.dma_start(out=hal2[1:2, :], in_=rsrc), writes=[B_hl2])
                    norm_T3(hal2, B_hl2, 0, sh1, hxq, B_hq, [(TH, 2, 0)])
                    lmask = cinfo[:, 32:33] if t0 == 0 else None
                    rmask = cinfo[:, 33:34] if t0 + TH == TL else None
                    fetch_job(0)
                    for cc in range(8):
                        fetch_job(cc + 1)
                        ws3, B_ws3 = ws3s[cc % 2]
                        pcs, bcs = psum()
                        phs, bhs = psum()
                        pbs, bbs = psum()
                        ph, bh = psum()
                        for (pp, bb_, pi) in ((pcs, bcs, 1), (phs, bhs, 2), (pbs, bbs, 0)):
                            for c in range(KD):
                                mm(pp[:, 0:TH], ws3[:, c, pi * 128:(pi + 1) * 128], hxq[:, c, 0:TH], c == 0, c == KD - 1,
                                   [B_ws3, B_hq], [bb_], inc=(c == KD - 1))
                        for pi in (1, 2):
                            for c in range(KD):
                                mm(ph[:, (pi - 1) * 2:(pi - 1) * 2 + 2], ws3[:, c, pi * 128:(pi + 1) * 128], hxq[:, c, TH:TH + 2],
                                   c == 0, c == KD - 1, [B_ws3, B_hq], [bh])
                        cp("act", scS[:, 0:TH], pcs[:, 0:TH], [bcs], [B_sc])
                        tt("dve", uext[:, 1:TH + 1], scS[:, 0:TH], phs[:, 0:TH], ALU.mult, [B_sc, bhs], [B_u])
                        cp("act", scS[:, 0:2], ph[:, 0:2], [bh, B_u], [B_sc])
                        tt("dve", uext[:, 0:1], scS[:, 0:1], ph[:, 2:3], ALU.mult, [B_sc, bh], [B_u])
                        tt("dve", uext[:, TH + 1:TH + 2], scS[:, 1:2], ph[:, 3:4], ALU.mult, [B_sc, bh], [B_u])
                        if lmask is not None:
                            ts("dve", uext[:, 0:1], uext[:, 0:1], lmask, None, ALU.mult, None, [B_u, B_ci], [B_u])
                        if rmask is not None:
                            ts("dve", uext[:, TH + 1:TH + 2], uext[:, TH + 1:TH + 2], rmask, None, ALU.mult, None, [B_u, B_ci], [B_u])
                        ts("dve", accS[:], uext[:, 0:TH], swt[:, cc, 0:1], None, ALU.mult, None, [B_u, B_small], [B_acc])
                        stt("dve", accS[:], uext[:, 1:TH + 1], swt[:, cc, 1:2], accS[:], ALU.mult, ALU.add, [B_u, B_small, B_acc], [B_acc])
                        stt("dve", accS[:], uext[:, 2:TH + 2], swt[:, cc, 2:3], accS[:], ALU.mult, ALU.add, [B_u, B_small, B_acc], [B_acc])
                        tt("dve", ybT[:, cc, :], accS[:], pbs[:, 0:TH], ALU.mult, [B_acc, bbs], [B_yb])
                    for m in range(KD):
                        if 8 + m + 1 < len(jobs3):
                            fetch_job(8 + m + 1)
                        ws3, B_ws3 = ws3s[(8 + m) % 2]
                        wp3, B_wp3 = wp3s[(8 + m) % 2]
                        pga, bga = psum()
                        pgb, bgb = psum()
                        ppa, bpa = psum()
                        ppb, bpb = psum()
                        for c in range(KD):
                            mm(pga[:, 0:TH], ws3[:, c, 0:128], hxq[:, c, 0:TH], c == 0, c == KD - 1, [B_ws3, B_hq], [bga], inc=(c == KD - 1))
                        for c in range(KD):
                            mm(pgb[:, 0:TH], ws3[:, c, 128:256], hxq[:, c, 0:TH], c == 0, c == KD - 1, [B_ws3, B_hq], [bgb], inc=(c == KD - 1))
                        for c in range(8):
                            mm(ppa[:, 0:TH], wp3[:, c, 0:128], yaq[:, c, :], c == 0, c == 7, [B_wp3, B_yaq], [bpa], inc=(c == 7))
                        for c in range(8):
                            mm(ppb[:, 0:TH], wp3[:, c, 128:256], ybT[:, c, :], c == 0, c == 7, [B_wp3, B_yb], [bpb], inc=(c == 7))
                        (sa, bsa), (sb_, bsb) = sgS
                        (ta, bta), (tb_, btb) = t12
                        act(sa[:, 0:TH], pga[:, 0:TH], AF.Sigmoid, [bga], [bsa])
                        act(sb_[:, 0:TH], pgb[:, 0:TH], AF.Sigmoid, [bgb], [bsb])
                        tt("dve", ta[:, 0:TH], ppa[:, 0:TH], sa[:, 0:TH], ALU.mult, [bpa, bsa], [bta])
                        tt("dve", tb_[:, 0:TH], ppb[:, 0:TH], sb_[:, 0:TH], ALU.mult, [bpb, bsb], [btb])
                        tt("pool", mgT[:, m, :], ta[:, 0:TH], tb_[:, 0:TH], ALU.add, [bta, btb], [B_mg])
                    for ng in range(4):
                        pp = [psum() for _ in range(nbq)]
                        for c2 in range(KD // 2):
                            wv, wb_ = wtile(wo_d, c2 * 2, 2, ng * 512, 512)
                            for i in range(nbq):
                                for ci in range(2):
                                    c = c2 * 2 + ci
                                    mm(pp[i][0][:], mgT[:, c, i * 128:(i + 1) * 128], wv[:, ci, :], c == 0, c == KD - 1,
                                       [B_mg, wb_], [pp[i][1]], inc=(c == KD - 1 or (i == nbq - 1 and ci == 1)))
                        for i in range(nbq):
                            xt, xb = xq[i]
                            tt("dve", tmpx[:], pp[i][0][:], g1bc[:, ng * 512:(ng + 1) * 512], ALU.mult, [pp[i][1], B_gbc], [B_tx])
                            tt("pool", xt[:, ng * 512:(ng + 1) * 512], xt[:, ng * 512:(ng + 1) * 512], tmpx[:], ALU.add, [xb, B_tx], [xb])
                    for i in range(nbq):
                        xt, xb = xq[i]
                        tbg = t0 // 128 + i
                        k.dma("sp", c_st, lambda e, xt=xt, tbg=tbg: e.dma_start(
                            out=xmid_d.ap()[tbg * 128:(tbg + 1) * 128, :], in_=xt[:]), reads=[xb], writes=[B_xmid[tbg]])
                        if tbg == 0:
                            k.dma("sp", c_st, lambda e, xt=xt: e.dma_start(out=hbin_d.ap()[0:64, :], in_=xt[0:64, :]),
                                  reads=[xb], writes=[B_hbin])
                        if tbg == NB - 1:
                            k.dma("sp", c_st, lambda e, xt=xt: e.dma_start(out=hbin_d.ap()[64:128, :], in_=xt[64:128, :]),
                                  reads=[xb], writes=[B_hbin])
                k.barrier()
            if stop == "S3":
                k.dead = True
            B_hbo = Buf("hbout")
            k.cc(c_cc, lambda e: e.collective_compute("AllGather", ALU.bypass, replica_groups=[list(range(NCORES))],
                                                      ins=[hbin_d.ap().opt()], outs=[hbout_d.ap().opt()]),
                 reads=[B_hbin], writes=[B_hbo])

            if stop == "S4":
                k.dead = True
            TF = min(512, TL)
            NT = TL // TF
            TE = TF + 128
            nbt = TF // 128
            with ExitStack() as s5:
                set_stg(s5, 4, 1024)
                xload5, norm_T5, xr5, (fj, B_fj) = mk_norm(s5, 4)
                wtile = mk_wring(s5, "act")
                xnS5 = None
                g2bc = k.sb("g2bc", [128, D], F32, s5)
                fgbc = k.sb("fgbc", [128, D], F32, s5)
                B_gbc = Buf("gbc5")
                k.dma("sp", c_misc, lambda e: e.dma_start(out=fgbc[:], in_=bcast(fg_d, D)), writes=[B_gbc])
                dg = k.sb("dg5", [128, 128], F32, s5)
                B_dg = Buf("dg5")
                for c4 in range(4):
                    pt, pb = psum()
                    for i in range(4):
                        c = c4 * 4 + i
                        ts("dve", dg[:], ident, modv[:, 80 + c, 0:1], None, ALU.mult, None, [B_cst, B_mod], [B_dg])
                        mm(pt[:, i * 128:(i + 1) * 128], ones, dg[:], True, True, [B_cst, B_dg], [pb])
                    cp("act", g2bc[:, c4 * 512:(c4 + 1) * 512], pt[:], [pb], [B_gbc])
                xhc = k.sb("xhc", [128, D], F32, s5)
                B_xhc = Buf("xhc")
                k.op("pool", lambda e: e.memset(xhc[:], 0.0), [], [B_xhc])
                for r in range(NCORES):
                    xt, xb, cx = xr5[r % 4]
                    k.dma("sp", cx, lambda e, xt=xt, r=r: e.dma_start(out=xt[0:64, :], in_=hbout_d.ap()[r * 128 + 64:r * 128 + 128, :]),
                          reads=[B_hbo], writes=[xb])
                    k.dma("sp", cx, lambda e, xt=xt, r=r: e.dma_start(out=xt[64:128, :], in_=hbout_d.ap()[r * 128:r * 128 + 64, :]),
                          reads=[B_hbo], writes=[xb])
                    stt("dve", xhc[:], xt[:], cinfo[:, 34 + r:35 + r], xhc[:], ALU.mult, ALU.add, [xb, B_ci, B_xhc], [B_xhc])
                h2T = k.sb("h2T", [128, KD, TE], BF16, s5)
                B_h2 = Buf("h2T")
                actT = k.sb("actT", [128, NFC, TF], BF16, s5)
                B_at = [Buf(f"actT{i}") for i in range(NFC)]
                wu = [(k.sb(f"wu{i}", [128, KD, 256], BF16, s5), Buf(f"wu{i}")) for i in range(3)]
                aS = [(k.sb(f"aS{i}", [128, 3, TE + 2], F32, s5), Buf(f"aS{i}")) for i in range(2)]
                cacc = [(k.sb(f"cacc{i}", [128, TF], F32, s5), Buf(f"cacc{i}")) for i in range(2)]
                cacc2 = [(k.sb(f"caccx{i}", [128, TF], F32, s5), Buf(f"caccx{i}")) for i in range(2)]
                cmask = k.sb("cmask", [128, 2, 66], F32, s5)
                B_cm = Buf("cmask")
                k.dma("sp", c_misc, lambda e: e.dma_start(out=cmask[:], in_=cmask_d.ap()[:, :, 0:66]), writes=[B_cm])
                for a3, ba in aS:
                    k.op("pool", lambda e, a3=a3: e.memset(a3[:], 0.0), [], [ba])
                tmpo, B_to = cacc2[0]
                fss = k.sb("fss", [128, 1], F32, s5)
                B_fss = Buf("fss")
                egrp = []
                o_ = 0
                while o_ < TE:
                    n_ = min(512, TE - o_)
                    egrp.append((o_, n_))
                    o_ += n_
                for it in range(NT):
                    t0 = it * TF
                    for tbl in range(nbt):
                        tbg = t0 // 128 + tbl
                        xt, xb = xload5(xmid_d.ap()[tbg * 128:(tbg + 1) * 128, :])
                        norm_T5(xt, xb, 2, sh2, h2T, B_h2, [(64 + tbl * 128, 128, 0)])
                    hal, B_hal, c_hal = xr5[0]
                    if t0 > 0:
                        k.dma("sp", c_hal, lambda e, t0=t0, hal=hal: e.dma_start(out=hal[0:64, :], in_=xmid_d.ap()[t0 - 64:t0, :]),
                              reads=B_xmid, writes=[B_hal])
                    else:
                        cp("pool", hal[0:64, :], xhc[0:64, :], [B_xhc], [B_hal])
                    if t0 + TF < TL:
                        k.dma("sp", c_hal, lambda e, t0=t0, hal=hal: e.dma_start(out=hal[64:128, :], in_=xmid_d.ap()[t0 + TF:t0 + TF + 64, :]),
                              reads=B_xmid, writes=[B_hal])
                    else:
                        cp("pool", hal[64:128, :], xhc[64:128, :], [B_xhc], [B_hal])
                    norm_T5(hal, B_hal, 2, sh2, h2T, B_h2, [(0, 64, 0), (64 + TF, 64, 64)])
                    pend = []

                    def fetch_issue(fc_):
                        wv_, wbb_ = wu[(it * NFC + fc_) % 3]
                        j_ = 0
                        for (c0_, o_) in ((fc_ * 128, 0), (DFF + fc_ * 128, 128)):
                            for hk in range(2):
                                t_, b_, c_ = stg[stgi[0]]
                                stgi[0] = (stgi[0] + 1) % len(stg)
                                sv_ = t_[:, 0:1024].rearrange("p (c n) -> p c n", c=8)
                                src_ = wsrc(wup_d, hk * 8, 8, c0_, 128)
                                k.dma(("sp", "act")[j_ % 2], c_, lambda e, sv_=sv_, src_=src_: e.dma_start(out=sv_, in_=src_), writes=[b_])
                                pend.append((wv_[:, hk * 8:(hk + 1) * 8, o_:o_ + 128], wbb_, sv_, b_))
                                j_ += 1

                    def fetch_cast():
                        for (dst_, dbuf_, sv_, b_) in pend:
                            cp("act", dst_, sv_, [b_], [dbuf_])
                        del pend[:]
                    fetch_issue(0)
                    fetch_cast()
                    fetch_issue(1)
                    fetch_cast()
                    for fc in range(NFC):
                        if fc + 2 < NFC:
                            fetch_issue(fc + 2)
                        wv, wb_ = wu[(it * NFC + fc) % 3]
                        a3, ba = aS[fc % 2]
                        ce = "pool"
                        for (o_, n_) in egrp:
                            pa, bpa = psum()
                            for c in range(KD):
                                mm(pa[:, 0:n_], wv[:, c, 0:128], h2T[:, c, o_:o_ + n_], c == 0, c == KD - 1, [wb_, B_h2], [bpa], inc=(c == KD - 1))
                            cp("act", a3[:, 0, 1 + o_:1 + o_ + n_], pa[:, 0:n_], [bpa], [ba])
                        if t0 == 0:
                            ts(ce, a3[:, 0, 1:65], a3[:, 0, 1:65], cinfo[:, 32:33], None, ALU.mult, None, [ba, B_ci], [ba])
                        if t0 + TF >= TL:
                            ts(ce, a3[:, 0, 65 + TF:129 + TF], a3[:, 0, 65 + TF:129 + TF], cinfo[:, 33:34], None, ALU.mult, None, [ba, B_ci], [ba])
                        a3v = lambda r_: a3[:, r_, 1:TE + 1].rearrange("p (g f) -> p g f", f=64)
                        mk_ = lambda r_: cmask[:, r_, 1:65].unsqueeze(1).to_broadcast([128, TE // 64, 64])
                        tt(ce, a3v(1), a3v(0), mk_(0), ALU.mult, [ba, B_cm], [ba])
                        tt(ce, a3v(2), a3v(0), mk_(1), ALU.mult, [ba, B_cm], [ba])
                        ca, bca = cacc[fc % 2]
                        accs = [(ca, bca), (cacc2[0][0], cacc2[0][1]), (cacc2[1][0], cacc2[1][1])]
                        for dx in (-1, 0, 1):
                            for dyi, dy in enumerate((-1, 0, 1)):
                                e0 = 65 + 64 * dy + dx
                                src = a3[:, (1, 0, 2)[dx + 1], e0:e0 + TF]
                                ti = (dy + 1) * 3 + (dx + 1)
                                wcol = dwt[:, fc, ti:ti + 1]
                                at_, ab_ = accs[dyi]
                                if dx == -1:
                                    ts("dve", at_[:], src, wcol, None, ALU.mult, None, [ba, B_small], [ab_])
                                else:
                                    stt("dve", at_[:], src, wcol, at_[:], ALU.mult, ALU.add, [ba, B_small, ab_], [ab_])
                        tt("pool", accs[1][0][:], accs[1][0][:], accs[2][0][:], ALU.add, [accs[1][1], accs[2][1]], [accs[1][1]])
                        tt("dve", ca[:], ca[:], accs[1][0][:], ALU.add, [bca, accs[1][1]], [bca])
                        act(ca[:], ca[:], AF.Silu, [bca, B_small], [bca], bias=dbt[:, fc:fc + 1])
                        pbb, bpb = psum()
                        for c in range(KD):
                            mm(pbb[:, 0:TF], wv[:, c, 128:256], h2T[:, c, 64:64 + TF], c == 0, c == KD - 1, [wb_, B_h2], [bpb], inc=(c == KD - 1))
                        tt("dve", actT[:, fc, :], ca[:], pbb[:, 0:TF], ALU.mult, [bca, bpb], [B_at[fc]])
                        fetch_cast()
                    xo = []
                    for i in range(nbt):
                        tbg = t0 // 128 + i
                        xo.append(xload5(xmid_d.ap()[tbg * 128:(tbg + 1) * 128, :]))
                    for ng in range(4):
                        pp = [psum() for _ in range(nbt)]
                        for k2 in range(NFC // 2):
                            wv, wb_ = wtile(wdn_d, k2 * 2, 2, ng * 512, 512)
                            for i in range(nbt):
                                for ci in range(2):
                                    kc = k2 * 2 + ci
                                    mm(pp[i][0][:], actT[:, kc, i * 128:(i + 1) * 128], wv[:, ci, :], kc == 0, kc == NFC - 1,
                                       [B_at[kc], wb_], [pp[i][1]], inc=(kc == NFC - 1 or (i == nbt - 1 and ci == 1)))
                        for i in range(nbt):
                            xt, xb = xo[i]
                            tt("dve", tmpo[:], pp[i][0][:], g2bc[:, ng * 512:(ng + 1) * 512], ALU.mult, [pp[i][1], B_gbc], [B_to])
                            tt("pool", xt[:, ng * 512:(ng + 1) * 512], xt[:, ng * 512:(ng + 1) * 512], tmpo[:], ALU.add, [xb, B_to], [xb])
                    for i in range(nbt):
                        xt, xb = xo[i]
                        tbg = t0 // 128 + i
                        act(fj[:], xt[:], AF.Square, [xb], [B_fj, B_fss], accum_out=fss[:])
                        rstd_from(fss[:], D, B_fss)
                        stt("dve", xt[:], xt[:], fss[:], fgbc[:], ALU.mult, ALU.mult, [xb, B_fss, B_gbc], [xb])
                        k.dma("sp", c_st, lambda e, xt=xt, tbg=tbg: e.dma_start(out=out_d.ap()[tbg * 128:(tbg + 1) * 128, :], in_=xt[:]),
                              reads=[xb], writes=[Buf("outrow")])

        except _Stop:
            raise
        k.dead = False
        k.barrier()
        k.emit()
    return nc


def _consts():
    c = np.zeros((128, 7, 128), np.float32)
    i = np.arange(128)
    same = (i[:, None] // 64) == (i[None, :] // 64)
    c[:, 0, :] = np.eye(128)
    c[:, 1, :] = same & (i[:, None] <= i[None, :])
    c[:, 2, :] = same & (i[:, None] >= i[None, :])
    c[:, 3, :] = same & (i[:, None] > i[None, :])
    c[:, 4, :] = same & (i[:, None] < i[None, :])
    c[:, 5, :] = 1.0
    return c


def _fp(v):
    return np.ascontiguousarray(v.reshape(-1, 128).T)


def make_in_maps(TL, x, c, ctx, c_ctx, w_mod, b_mod, norm1_g, w_in, lb_raw, a_norm_g, sconv_w,
                 w_pa, w_pb, w_o, norm2_g, w_up, ffn_dw, ffn_db, w_down, final_g):
    f = lambda a: np.ascontiguousarray(np.asarray(a, dtype=np.float32))
    x2 = f(x)[0]
    S = x2.shape[0]
    assert S == TL * NCORES
    wm = f(w_mod)[0]
    bmodv = f(b_mod)[0]
    cvec = np.stack([_fp(f(c)[0]), _fp(f(c_ctx))], axis=-1)
    cmask = np.ones((128, 2, 1200), np.float32)
    e = np.arange(1200)
    gcol = (e - 1) % 64
    cmask[:, 0, gcol == 63] = 0.0
    cmask[:, 1, gcol == 0] = 0.0
    shared = {
        "ctx": f(ctx)[0], "cvec": np.ascontiguousarray(cvec), "n1g": _fp(f(norm1_g)[0]), "n2g": _fp(f(norm2_g)[0]),
        "fg": f(final_g).reshape(1, D), "w_in": f(w_in)[0], "lbraw": f(lb_raw).reshape(1, -1),
        "ang": f(a_norm_g).reshape(1, 128),
        "sconvw": np.ascontiguousarray(f(sconv_w)[0].reshape(3, 8, 128).transpose(2, 1, 0)),
        "w_pa": f(w_pa)[0], "w_pb": f(w_pb)[0], "w_o": f(w_o)[0], "w_up": f(w_up)[0],
        "ffndw": np.ascontiguousarray(f(ffn_dw)[0].reshape(9, NFC, 128).transpose(2, 1, 0)),
        "ffndb": _fp(f(ffn_db)[0]), "w_down": f(w_down)[0], "consts": _consts(), "cmask": cmask,
    }
    maps = []
    for r in range(NCORES):
        m = dict(shared)
        m["x"] = np.ascontiguousarray(x2[r * TL:(r + 1) * TL])
        xh = np.zeros((128, D), np.float32)
        if r > 0:
            xh[0] = x2[r * TL - 1]
        if r < NCORES - 1:
            xh[1] = x2[(r + 1) * TL]
        m["xh"] = xh
        m["wmod"] = np.ascontiguousarray(wm[:, r * 1536:(r + 1) * 1536])
        m["bmod"] = _fp(bmodv[r * 1536:(r + 1) * 1536])
        ci = np.zeros((128, 48), np.float32)
        for q in range(NCORES):
            ci[:, q] = 1.0 if q < r else 0.0
            ci[:, 8 + q] = 1.0 if q > r else 0.0
            ci[:, 16 + q] = 1.0 - ci[:, q]
            ci[:, 24 + q] = 1.0 - ci[:, 8 + q]
            ci[0:64, 34 + q] = 1.0 if q == r - 1 else 0.0
            ci[64:128, 34 + q] = 1.0 if q == r + 1 else 0.0
        ci[:, 32] = 1.0 if r > 0 else 0.0
        ci[:, 33] = 1.0 if r < NCORES - 1 else 0.0
        m["cinfo"] = ci
        maps.append(m)
    return maps


_CACHE = {}


def kernel(**inputs):
    S = inputs["x"].shape[1]
    TL = S // NCORES
    if TL not in _CACHE:
        _CACHE[TL] = build_program(TL)
    nc = _CACHE[TL]
    maps = make_in_maps(TL, **inputs)
    res = run_bass_kernel_spmd(nc, maps, core_ids=list(range(NCORES)))
    out = np.concatenate([res.results[r]["out"] for r in range(NCORES)], axis=0)
    return out.reshape(1, S, D).astype(np.float32)
```

```python
import numpy as np
from contextlib import ExitStack
import concourse.bass as bass
import concourse.mybir as mybir
from concourse.bass_utils import run_bass_kernel_spmd

F32 = mybir.dt.float32
BF16 = mybir.dt.bfloat16
AF = mybir.ActivationFunctionType
ALU = mybir.AluOpType

D = 2048
KD = 16
AW = 1024
NH = 8
DFF = 5632
NFC = 44
INC = 12288
EPS = 1e-6
NCORES = 8
CTXL = 256


class Ctr:
    def __init__(self, h, name):
        self.h = h
        self.n = 0
        self.name = name


class Buf:
    __slots__ = ("name", "w", "r")

    def __init__(self, name=""):
        self.name = name
        self.w = None
        self.r = {}


class K:
    ENG = ("pe", "act", "dve", "pool", "sp")

    def __init__(self, nc, es):
        self.nc = nc
        self.es = es
        self.prog = {e: [] for e in self.ENG}
        self.ctr = {e: Ctr(nc.alloc_semaphore(name="c_" + e), e) for e in self.ENG}
        self.seen = {e: {} for e in self.ENG}
        self.allctr = list(self.ctr.values())
        self.uid = 0
        self.dead = False

    def sb(self, name, shape, dt, es=None):
        self.uid += 1
        return (es or self.es).enter_context(self.nc.sbuf_tensor(f"s{self.uid}_{name}", list(shape), dt))

    def ps(self, name, shape, dt=F32):
        return self.es.enter_context(self.nc.psum_tensor("p_" + name, list(shape), dt))

    def newctr(self, name):
        c = Ctr(self.nc.alloc_semaphore(name=name), name)
        self.allctr.append(c)
        return c

    def _wait(self, e, tk):
        if tk is None:
            return
        c, v = tk
        if c.name not in self.ENG:
            v = c.n
        s = self.seen[e]
        if s.get(c, 0) >= v:
            return
        s[c] = v
        self.prog[e].append(("w", c.h, v))

    def _deps(self, e, reads, writes, strict=False):
        me = None if strict else self.ctr[e]
        for b in reads:
            if b.w is not None and not (b.w[0] is me and e == "pe"):
                self._wait(e, b.w)
        for b in writes:
            if b.w is not None and not (b.w[0] is me and e == "pe"):
                self._wait(e, b.w)
            for c, v in b.r.items():
                if c is me:
                    continue
                self._wait(e, (c, v))

    def _mark(self, tk, reads, writes):
        for b in reads:
            if b.r.get(tk[0], 0) < tk[1]:
                b.r[tk[0]] = tk[1]
        for b in writes:
            b.w = tk
            b.r = {}

    def op(self, e, fn, reads=(), writes=(), inc=True):
        if self.dead:
            return None
        self._deps(e, reads, writes)
        c = self.ctr[e]
        if inc:
            c.n += 1
            tk = (c, c.n)
            self.prog[e].append(("o", fn, c.h, 1))
        else:
            tk = (c, c.n + 1)
            self.prog[e].append(("o", fn, None, 0))
        self._mark(tk, reads, writes)
        return tk

    def dma(self, q, ctr, fn, reads=(), writes=()):
        if self.dead:
            return None
        self._deps(q, reads, writes, strict=True)
        ctr.n += 16
        tk = (ctr, ctr.n)
        self.prog[q].append(("o", fn, ctr.h, 16))
        self._mark(tk, reads, writes)
        return tk

    def cc(self, ctr, fn, reads=(), writes=()):
        if self.dead:
            return None
        self._deps("pool", reads, writes)
        ctr.n += 1
        tk = (ctr, ctr.n)
        self.prog["pool"].append(("o", fn, ctr.h, 1))
        self._mark(tk, reads, writes)
        return tk

    def barrier(self):
        if self.dead:
            return
        for e in self.ENG:
            for c in self.allctr:
                if c.n > 0 and c is not self.ctr[e]:
                    self._wait(e, (c, c.n))

    def emit(self):
        nc = self.nc
        with nc.Block() as block:
            def run(name):
                def body(eng):
                    for it in self.prog[name]:
                        if it[0] == "w":
                            eng.wait_ge(it[1], it[2])
                        else:
                            ins = it[1](eng)
                            if it[2] is not None:
                                ins.then_inc(it[2], it[3])
                return body
            block.sync(run("sp"))
            block.tensor(run("pe"))
            block.scalar(run("act"))
            block.vector(run("dve"))
            block.gpsimd(run("pool"))


class _Stop(Exception):
    pass


def build_program(TL, dbg=None, stop=None):
    NB = TL // 128
    NCB = CTXL // 128
    nc = bass.Bass("TRN2", target_bir_lowering=False)

    def din(name, shape):
        return nc.dram_tensor(name, list(shape), F32, kind="ExternalInput")

    x_d = din("x", [TL, D])
    xh_d = din("xh", [128, D])
    ctx_d = din("ctx", [CTXL, D])
    wmod_d = din("wmod", [D, 1536])
    bmod_d = din("bmod", [128, 12])
    cvec_d = din("cvec", [128, KD, 2])
    n1g_d = din("n1g", [128, KD])
    n2g_d = din("n2g", [128, KD])
    fg_d = din("fg", [1, D])
    win_d = din("w_in", [D, INC])
    lb_d = din("lbraw", [1, 2 * 2 * AW])
    ang_d = din("ang", [1, 128])
    sw_d = din("sconvw", [128, 8, 3])
    wpa_d = din("w_pa", [AW, D])
    wpb_d = din("w_pb", [AW, D])
    wo_d = din("w_o", [D, D])
    wup_d = din("w_up", [D, 2 * DFF])
    dw_d = din("ffndw", [128, NFC, 9])
    db_d = din("ffndb", [128, NFC])
    wdn_d = din("w_down", [DFF, D])
    cst_d = din("consts", [128, 7, 128])
    cinfo_d = din("cinfo", [128, 48])
    cmask_d = din("cmask", [128, 2, 1200])
    out_d = nc.dram_tensor("out", [TL, D], F32, kind="ExternalOutput")
    dbg_d = {}
    if dbg:
        for nm, shp in dbg.items():
            dbg_d[nm] = nc.dram_tensor("dbg_" + nm, list(shp), F32, kind="ExternalOutput")

    modin_d = nc.dram_tensor("modin", [128, 24], F32)
    modout_d = nc.dram_tensor("modout", [128 * NCORES, 24], F32)
    stin_d = [nc.dram_tensor(f"stin{h}", [128, 258], F32) for h in range(NH)]
    stout_d = [nc.dram_tensor(f"stout{h}", [128 * NCORES, 258], F32) for h in range(NH)]
    xmid_d = nc.dram_tensor("xmid", [TL, D], F32)
    ya_d = nc.dram_tensor("ya_scr", [128, NH, TL], BF16)
    hbin_d = nc.dram_tensor("hbin", [128, D], F32)
    hbout_d = nc.dram_tensor("hbout", [128 * NCORES, D], F32)

    def bcast(h, n, off=0):
        return bass.AP(h, off, [[0, 128], [1, n]])

    with ExitStack() as es:
        k = K(nc, es)
        c_const = k.newctr("ld_const")
        c_cc = k.newctr("cc")
        c_misc = k.newctr("ld_misc")
        c_st = k.newctr("st")

        if dbg:
            dbgbuf = k.sb("dbgbuf", [128, 2048], F32)
            B_dbg = Buf("dbg")
        cst = k.sb("cst", [128, 7, 128], F32)
        B_cst = Buf("cst")
        ident = cst[:, 0, :]
        U = [cst[:, 1, :], cst[:, 2, :]]
        Wm = [cst[:, 3, :], cst[:, 4, :]]
        ones = cst[:, 5, :]
        cinfo = k.sb("cinfo", [128, 48], F32)
        B_ci = Buf("cinfo")
        modv = k.sb("modv", [128, 96, 2], F32)
        B_mod = Buf("modv")
        gm = k.sb("gm", [128, 3, KD], F32)
        B_gm = Buf("gm")
        n1g = k.sb("n1g", [128, KD], F32)
        n2g = k.sb("n2g", [128, KD], F32)
        swt = k.sb("swt", [128, 8, 3], F32)
        dwt = k.sb("dwt", [128, NFC, 9], F32)
        dbt = k.sb("dbt", [128, NFC], F32)
        ang = k.sb("angbc", [128, 128], F32)
        epsc = k.sb("epsc", [128, 1], F32)
        B_small = Buf("small")

        for (t, src) in ((cst[:], cst_d.ap()), (cinfo[:], cinfo_d.ap()), (n1g[:], n1g_d.ap()), (n2g[:], n2g_d.ap()),
                         (swt[:], sw_d.ap()), (dwt[:], dw_d.ap()), (dbt[:], db_d.ap()), (ang[:], bcast(ang_d, 128))):
            k.dma("sp", c_const, lambda e, t=t, src=src: e.dma_start(out=t, in_=src), writes=[B_cst, B_ci, B_small])

        psr = [(k.ps(f"ps{i}", [128, 512], F32), Buf(f"ps{i}")) for i in range(8)]
        psi = [0]

        def psum():
            i = psi[0]
            psi[0] = (i + 1) % 8
            return psr[i]

        STG = 1024
        stg = []
        stgi = [0]
        STGSZ = [1024]
        casti = [0]
        dualq = [False]
        stg_ctrs = [k.newctr(f"ldw{i}") for i in range(4)]

        def set_stg(scope, n, size):
            del stg[:]
            for i in range(n):
                stg.append((k.sb(f"stg{i}", [128, size], F32, scope), Buf(f"stg{i}"), stg_ctrs[i]))
            stgi[0] = 0
            STGSZ[0] = size

        def fetch(dst, dstbuf, src, kc, ncol, cast_eng=None):
            if kc * ncol > STGSZ[0]:
                h_ = kc // 2
                fetch(dst[:, 0:h_, :], dstbuf, src[:, 0:h_, :], h_, ncol, cast_eng)
                fetch(dst[:, h_:kc, :], dstbuf, src[:, h_:kc, :], kc - h_, ncol, cast_eng)
                return
            t, b, c = stg[stgi[0]]
            stgi[0] = (stgi[0] + 1) % len(stg)
            sv = t[:, 0:kc * ncol].rearrange("p (c n) -> p c n", c=kc)
            q_ = "sp"
            if cast_eng is None:
                cast_eng = ("act", "dve")[casti[0] % 2]
                if dualq[0] and casti[0] % 2 == 1:
                    q_ = "act"
                casti[0] += 1
            k.dma(q_, c, lambda e: e.dma_start(out=sv, in_=src), writes=[b])
            cp(cast_eng, dst, sv, [b], [dstbuf])

        def wsrc(wd, r0c, kc, c0, ncol):
            return wd.ap()[r0c * 128:(r0c + kc) * 128, c0:c0 + ncol].rearrange("(c p) n -> p c n", p=128)

        def mk_wring(scope, cast_eng=None):
            NWR = 4
            wring = [(k.sb(f"wr{i}", [128, STG], BF16, scope), Buf(f"wr{i}")) for i in range(NWR)]
            wri = [0]

            def wtile(wd, r0c, kc, c0, ncol):
                t, b = wring[wri[0]]
                wri[0] = (wri[0] + 1) % NWR
                v = t[:, 0:kc * ncol].rearrange("p (c n) -> p c n", c=kc)
                fetch(v, b, wsrc(wd, r0c, kc, c0, ncol), kc, ncol, cast_eng)
                return v, b
            return wtile

        def act(out, in_, func, reads, writes, **kw):
            return k.op("act", lambda e: e.activation(out=out, in_=in_, func=func, **kw), reads, writes)

        def tt(eng, out, a, b, op, reads, writes):
            return k.op(eng, lambda e: e.tensor_tensor(out=out, in0=a, in1=b, op=op), reads, writes)

        def ts(eng, out, a, s1, s2, op0, op1, reads, writes):
            if s2 is None:
                return k.op(eng, lambda e: e.tensor_scalar(out=out, in0=a, scalar1=s1, scalar2=None, op0=op0), reads, writes)
            return k.op(eng, lambda e: e.tensor_scalar(out=out, in0=a, scalar1=s1, scalar2=s2, op0=op0, op1=op1), reads, writes)

        def stt(eng, out, a, s, b, op0, op1, reads, writes):
            return k.op(eng, lambda e: e.scalar_tensor_tensor(out=out, in0=a, scalar=s, in1=b, op0=op0, op1=op1), reads, writes)

        def cp(eng, out, in_, reads, writes):
            if eng == "act":
                return k.op("act", lambda e: e.copy(out=out, in_=in_), reads, writes)
            return k.op(eng, lambda e: e.tensor_copy(out=out, in_=in_), reads, writes)

        def mm(out, lhsT, rhs, start, stop, reads, writes, inc=None):
            return k.op("pe", lambda e: e.matmul(out, lhsT=lhsT, rhs=rhs, start=start, stop=stop), reads, writes,
                        inc=(True if inc is None else inc))

        def tr(out, in_, reads, writes, inc=True):
            return k.op("pe", lambda e: e.transpose(out=out, in_=in_, identity=ident), list(reads) + [B_cst], writes, inc=inc)

        def rstd_from(ssq, n, tmpbuf):
            act(ssq, ssq, AF.Ln, [tmpbuf, B_small], [tmpbuf], scale=1.0 / n, bias=epsc[:])
            act(ssq, ssq, AF.Exp, [tmpbuf], [tmpbuf], scale=-0.5)

        def dump(name, ap, buf):
            if name in dbg_d:
                k.dma("sp", c_st, lambda e: e.dma_start(out=dbg_d[name].ap(), in_=ap), reads=[buf])

        try:
            with ExitStack() as s0:
                cv = k.sb("cv", [128, KD, 2], F32, s0)
                B_cv = Buf("cv")
                k.dma("sp", c_misc, lambda e: e.dma_start(out=cv[:], in_=cvec_d.ap()), writes=[B_cv])
                act(cv[:], cv[:], AF.Silu, [B_cv], [B_cv])
                bm = k.sb("bm", [128, 12], F32, s0)
                k.dma("sp", c_misc, lambda e: e.dma_start(out=bm[:], in_=bmod_d.ap()), writes=[B_cv])
                wm = k.sb("wmf", [128, KD, 512], F32, s0)
                B_wm = Buf("wm")
                mloc = k.sb("mloc", [128, 12, 2], F32, s0)
                B_ml = Buf("mloc")
                for g3 in range(3):
                    k.dma("sp", c_misc, lambda e, g3=g3: e.dma_start(
                        out=wm[:], in_=wmod_d.ap()[:, g3 * 512:(g3 + 1) * 512].rearrange("(c p) n -> p c n", p=128)),
                        writes=[B_wm])
                    for j4 in range(4):
                        j = g3 * 4 + j4
                        pt, pb = psum()
                        for c in range(KD):
                            mm(pt[:, 0:2], wm[:, c, j4 * 128:(j4 + 1) * 128], cv[:, c, :], c == 0, c == KD - 1,
                               [B_wm, B_cv], [pb])
                        ts("dve", mloc[:, j, :], pt[:, 0:2], bm[:, j:j + 1], None, ALU.add, None, [pb, B_cv], [B_ml])
                k.dma("sp", c_misc, lambda e: e.dma_start(out=modin_d.ap(), in_=mloc[:].rearrange("p j t -> p (j t)")),
                      reads=[B_ml], writes=[B_ml])
                B_mo = Buf("modout")
                k.cc(c_cc, lambda e: e.collective_compute("AllGather", ALU.bypass, replica_groups=[list(range(NCORES))],
                                                          ins=[modin_d.ap().opt()], outs=[modout_d.ap().opt()]),
                     reads=[B_ml], writes=[B_mo])
                k.dma("sp", c_misc, lambda e: e.dma_start(
                    out=modv[:].rearrange("p (r j) t -> p r (j t)", r=NCORES),
                    in_=modout_d.ap().rearrange("(r p) f -> p r f", p=128)), reads=[B_mo], writes=[B_mod])
                stt("dve", gm[:, 0, :], modv[:, 16:32, 0], 1.0, n1g[:], ALU.add, ALU.mult, [B_mod, B_small], [B_gm])
                stt("dve", gm[:, 1, :], modv[:, 16:32, 1], 1.0, n1g[:], ALU.add, ALU.mult, [B_mod, B_small], [B_gm])
                stt("dve", gm[:, 2, :], modv[:, 64:80, 0], 1.0, n2g[:], ALU.add, ALU.mult, [B_mod, B_small], [B_gm])
                k.op("dve", lambda e: e.memset(epsc[:], EPS), [], [B_small])
                k.barrier()
                if stop == "S0":
                    k.dead = True
            sh1 = lambda c: modv[:, 0 + c, 0:1]
            shc = lambda c: modv[:, 0 + c, 1:2]
            sh2 = lambda c: modv[:, 48 + c, 0:1]

            def mk_norm(scope, nx):
                xring = [(k.sb(f"xr{i}", [128, D], F32, scope), Buf(f"xr{i}"), k.newctr(f"ldx{len(k.allctr)}")) for i in range(nx)]
                xri = [0]
                xnS = k.sb("xnS", [128, D], F32, scope)
                B_xn = Buf("xnS")
                nss = k.sb("nrmss", [128, 1], F32, scope)
                B_nss = Buf("nss")

                def xload(src_ap, rows=None):
                    t, b, c = xring[xri[0]]
                    xri[0] = (xri[0] + 1) % nx
                    dst = t[:] if rows is None else t[rows[0]:rows[1], :]
                    k.dma("sp", c, lambda e: e.dma_start(out=dst, in_=src_ap), writes=[b])
                    return t, b

                def norm_T(xt, xb, gsel, shf, dstT, dstbuf, pieces):
                    act(xnS[:], xt[:], AF.Square, [xb], [B_xn, B_nss], accum_out=nss[:])
                    rstd_from(nss[:], D, B_nss)
                    act(xnS[:], xt[:], AF.Copy, [xb, B_nss], [B_xn], scale=nss[:])
                    for c4 in range(4):
                        pt, pb = psum()
                        for i in range(4):
                            c = c4 * 4 + i
                            tr(pt[:, i * 128:(i + 1) * 128], xnS[:, c * 128:(c + 1) * 128], [B_xn], [pb], inc=(i == 3))
                        for i in range(4):
                            c = c4 * 4 + i
                            for (col0, ncols, pcol0) in pieces:
                                act(dstT[:, c, col0:col0 + ncols], pt[:, i * 128 + pcol0:i * 128 + pcol0 + ncols], AF.Identity,
                                    [pb, B_gm, B_mod], [dstbuf], scale=gm[:, gsel, c:c + 1], bias=shf(c))
                return xload, norm_T, xring, (xnS, B_xn)

            B_ya = Buf("ya_scr")
            c_ya = k.newctr("st_ya")
            sHX = ExitStack()
            es.enter_context(sHX)
            hxT = k.sb("hxT", [128, KD, TL], BF16, sHX)
            B_hx = [Buf(f"hx{i}") for i in range(NB)]
            hcT = k.sb("hcT", [128, KD, CTXL], BF16, sHX)
            B_hc = [Buf(f"hc{i}") for i in range(NCB)]
            with ExitStack() as s1:
                xload, norm_T, _, _ = mk_norm(s1, 2)
                for tb in range(NB):
                    xt, xb = xload(x_d.ap()[tb * 128:(tb + 1) * 128, :])
                    norm_T(xt, xb, 0, sh1, hxT, B_hx[tb], [(tb * 128, 128, 0)])
                for tb in range(NCB):
                    xt, xb = xload(ctx_d.ap()[tb * 128:(tb + 1) * 128, :])
                    norm_T(xt, xb, 1, shc, hcT, B_hc[tb], [(tb * 128, 128, 0)])
                k.barrier()
                if stop == "S1":
                    k.dead = True
            with ExitStack() as s2:
                set_stg(s2, 3, 1024)
                G = min(4, NB)
                NCH = 2 * NB
                ctxS_t = [k.sb(f"ctxS{d}", [128, 128], F32, s2) for d in range(2)]
                whs = [(k.sb(f"wh{i}", [128, KD, 384], BF16, s2), Buf(f"wh{i}")) for i in range(2)]
                wh, B_wh = whs[0]
                whq = k.sb("whq", [128, KD, 128], BF16, s2)
                B_whq = Buf("whq")

                def fetch_wh(h_, d_):
                    wv_, wb__ = whs[(2 * h_ + d_) % 2]
                    if d_ == 0:
                        cols_ = (AW + h_ * 128, 3 * AW + h_ * 128, 4 * AW + h_ * 128)
                    else:
                        cols_ = (2 * AW + h_ * 128, 3 * AW + h_ * 128)
                    for pi, c0 in enumerate(cols_):
                        fetch(wv_[:, :, pi * 128:(pi + 1) * 128], wb__, wsrc(win_d, 0, KD, c0, 128), KD, 128)

                def fetch_q(h_):
                    fetch(whq[:], B_whq, wsrc(win_d, 0, KD, h_ * 128, 128), KD, 128)
                qsT = k.sb("qsT", [128, TL], BF16, s2)
                B_qs = Buf("qsT")
                qdT = k.sb("qdT", [128, 2, TL], BF16, s2)
                B_qd = [Buf("qd0"), Buf("qd1")]
                opart = k.sb("opart", [128, NB, 128], F32, s2)
                B_op = Buf("opart")
                ggS = k.sb("ggS", [128, NB, 128], BF16, s2)
                B_gg = Buf("gg")
                KVs = k.sb("KVs", [128, NCH, 128], F32, s2)
                B_KV = Buf("KVs")
                Sbf = k.sb("Sbf", [128, NCH, 128], BF16, s2)
                B_Sbf = Buf("Sbf")
                Zs = k.sb("Zs", [128, 128], BF16, s2)
                Acol = k.sb("Acol", [128, NCH], F32, s2)
                B_A = Buf("Acol")
                PcJ = k.sb("PcJ", [128, NCH], F32, s2)
                Pend = k.sb("Pend", [128, 1], F32, s2)
                B_Pc = Buf("Pc")
                lbt = k.sb("lbt", [128, 2, 2, 128], F32, s2)
                lbv = lbt
                B_lb = Buf("lb")
                summ = k.sb("summ", [128, 2, 129], F32, s2)
                B_sum = Buf("summ")
                agT = k.sb("agT", [128, NCORES, 129], F32, s2)
                B_ag = Buf("agT")
                Sin = k.sb("Sin", [128, 2, 128], BF16, s2)
                B_Sin = Buf("Sin")
                pm8 = k.sb("pm8", [128, NCORES], F32, s2)
                B_pm8 = Buf("pm8")
                tmpS = k.sb("tmpS", [128, 128], F32, s2)
                B_tmp = Buf("tmpS")
                wkt = {}
                for nm in ("fG", "lgG", "kSG", "ebG", "enbG", "eeG", "tmpG"):
                    wkt[nm] = (k.sb(f"wk_{nm}", [128, G, 128], F32, s2), Buf(nm))
                for nm in ("vSG", "kdG", "scmG"):
                    wkt[nm] = (k.sb(f"wk_{nm}", [128, G, 128], BF16, s2), Buf(nm))
                kdzG = k.sb("wk_kdz", [128, G, 2, 128], BF16, s2)
                B_kdz = Buf("kdz")
                ssG = k.sb("wk_ss", [128, G], F32, s2)
                B_ss = Buf("ssG")
                k.op("pool", lambda e: e.memset(kdzG[:], 0.0), [], [B_kdz])
                ystg = [(k.sb("ystg0", [128, G * 128], BF16, s2), Buf("ystg0"))] * 2
                ysi = [0]
                k.op("pool", lambda e: e.memset(Zs[:], 0.0), [], [B_Sbf])
                fG, B_f = wkt["fG"]
                lgG, B_lg = wkt["lgG"]
                kSG, B_kS = wkt["kSG"]
                lgGs = [wkt["lgG"], (k.sb("wk_lgG2", [128, G, 128], F32, s2), Buf("lgG2"))]
                kSGs = [wkt["kSG"], (k.sb("wk_kSG2", [128, G, 128], F32, s2), Buf("kSG2"))]
                vSGs = [wkt["vSG"], (k.sb("wk_vSG2", [128, G, 128], BF16, s2), Buf("vSG2"))]
                ebG, B_eb = wkt["ebG"]
                enbG, B_enb = wkt["enbG"]
                eeG, B_ee = wkt["eeG"]
                tmpG, B_tg = wkt["tmpG"]
                vSG, B_v = wkt["vSG"]
                kdG, B_kd = wkt["kdG"]
                scmG, B_scm = wkt["scmG"]

                def bc3(ap2, g):
                    return ap2.unsqueeze(1).to_broadcast([128, g, 128])

                def sig_inplace(ap, buf):
                    act(ap, ap, AF.Ln, [buf, B_cst], [buf], bias=ones[:, 0:1])
                    act(ap, ap, AF.Exp, [buf], [buf], scale=-1.0)

                def sweep(h, d, actT_, B_act, nblk, is_ctx):
                    nz = 384 if (d == 0 and not is_ctx) else 256
                    gate = (d == 0 and not is_ctx)
                    Gs = min(G, nblk)
                    nch = 2 * nblk
                    KV4 = KVs[:, 0:nch, :].rearrange("p (b c) f -> p b c f", c=2)
                    A3 = Acol[:, 0:nch].rearrange("p (b c) -> p b c", c=2)
                    GW = Gs * 128

                    def a1(gi):
                        b0 = gi * Gs
                        lgG, B_lg = lgGs[gi % 2]
                        kSG, B_kS = kSGs[gi % 2]
                        vSG, B_v = vSGs[gi % 2]
                        Z = []
                        for i in range(Gs):
                            tb = b0 + i
                            zt, zb = psum()
                            for c in range(KD):
                                mm(zt[:, 0:nz], actT_[:, c, tb * 128:(tb + 1) * 128], wh[:, c, 0:nz], c == 0, c == KD - 1,
                                   [B_act[tb], B_wh], [zb], inc=(c == KD - 1))
                            Z.append((zt, zb))
                        for i, (zt, zb) in enumerate(Z):
                            act(fG[:, i, :], zt[:, 0:128], AF.Exp, [zb], [B_f], scale=-1.0)
                            cp("act", vSG[:, i, :], zt[:, 128:256], [zb], [B_v])
                            if gate:
                                act(tmpG[:, i, :], zt[:, 256:384], AF.Exp, [zb], [B_tg], scale=-1.0)
                        fv = fG[:, 0:Gs, :]
                        sig_inplace(fv, B_f)
                        tt("dve", fv, fv, bc3(lbv[:, 1, d, :], Gs), ALU.mult, [B_f, B_lb], [B_f])
                        tt("dve", fv, fv, bc3(lbv[:, 0, d, :], Gs), ALU.add, [B_f, B_lb], [B_f])
                        act(lgG[:, 0:Gs, :], fv, AF.Ln, [B_f], [B_lg])
                        ts("dve", kSG[:, 0:Gs, :], fv, -1.0, 1.0, ALU.mult, ALU.add, [B_f], [B_kS])
                        if gate:
                            sig_inplace(tmpG[:, 0:Gs, :], B_tg)
                            for i, (zt, zb) in enumerate(Z):
                                tt("dve", tmpG[:, i, :], tmpG[:, i, :], zt[:, 256:384], ALU.mult, [B_tg, zb], [B_tg])
                            tt("pool", ggS[:, b0:b0 + Gs, :], tmpG[:, 0:Gs, :], bc3(ang[:], Gs), ALU.mult, [B_tg, B_small], [B_gg])

                    def a2(gi):
                        b0 = gi * Gs
                        tok0 = b0 * 128
                        lgG, B_lg = lgGs[gi % 2]
                        kSG, B_kS = kSGs[gi % 2]
                        vSG, B_v = vSGs[gi % 2]
                        Pb, bPb = psum()
                        Pe, bPe = psum()
                        Pk, bPk = psum()
                        for i in range(Gs):
                            cs = slice(i * 128, (i + 1) * 128)
                            mm(Pb[:, cs], lgG[:, i, :], U[d], True, True, [B_lg, B_cst], [bPb], inc=(i == Gs - 1))
                        for i in range(Gs):
                            cs = slice(i * 128, (i + 1) * 128)
                            mm(Pe[:, cs], Wm[d], lgG[:, i, :], True, True, [B_lg, B_cst], [bPe], inc=(i == Gs - 1))
                        if not is_ctx:
                            for i in range(Gs):
                                cs = slice(i * 128, (i + 1) * 128)
                                tr(Pk[:, cs], kSG[:, i, :], [B_kS], [bPk], inc=(i == Gs - 1))
                        Pb3 = Pb[:, 0:GW].rearrange("p (g f) -> p g f", g=Gs)
                        Pe3 = Pe[:, 0:GW].rearrange("p (g f) -> p g f", g=Gs)
                        Pk3 = Pk[:, 0:GW].rearrange("p (g f) -> p g f", g=Gs)
                        act(ebG[:, 0:Gs, :], Pb3, AF.Exp, [bPb], [B_eb])
                        act(eeG[:, 0:Gs, :], Pe3, AF.Exp, [bPe], [B_ee])
                        tt("pool", kdzG[0:64, 0:Gs, 0, :], kSG[0:64, 0:Gs, :], eeG[0:64, 0:Gs, :], ALU.mult, [B_kS, B_ee], [B_kdz])
                        tt("pool", kdzG[64:128, 0:Gs, 1, :], kSG[64:128, 0:Gs, :], eeG[64:128, 0:Gs, :], ALU.mult, [B_kS, B_ee], [B_kdz])
                        for cidx in range(2):
                            acol = (cidx * 64 + 63) if d == 0 else cidx * 64
                            cp("pool", A3[:, b0:b0 + Gs, cidx], ebG[:, 0:Gs, acol], [B_eb], [B_A])
                        if not is_ctx:
                            act(enbG[:, 0:Gs, :], Pb3, AF.Exp, [bPb], [B_enb], scale=-1.0)
                            qd3 = qdT[:, d, tok0:tok0 + GW].rearrange("p (g f) -> p g f", g=Gs)
                            qs3 = qsT[:, tok0:tok0 + GW].rearrange("p (g f) -> p g f", g=Gs)
                            tt("dve", qd3, qs3, ebG[:, 0:Gs, :], ALU.mult, [B_qs, B_eb], [B_qd[d]])
                            tt("dve", kdG[:, 0:Gs, :], Pk3, enbG[:, 0:Gs, :], ALU.mult, [bPk, B_enb], [B_kd])
                            Ps, bPs = psum()
                            for i in range(Gs):
                                tb = b0 + i
                                mm(Ps[:, i * 128:(i + 1) * 128], kdG[:, i, :], qdT[:, d, tb * 128:(tb + 1) * 128], True, True,
                                   [B_kd, B_qd[d]], [bPs], inc=(i == Gs - 1))
                            Ps3 = Ps[:, 0:GW].rearrange("p (g f) -> p g f", g=Gs)
                            tt("dve", scmG[:, 0:Gs, :], Ps3, bc3(U[d], Gs), ALU.mult, [bPs, B_cst], [B_scm])
                            Po, bPo = psum()
                            for i in range(Gs):
                                mm(Po[:, i * 128:(i + 1) * 128], scmG[:, i, :], vSG[:, i, :], True, True, [B_scm, B_v], [bPo],
                                   inc=(i == Gs - 1))
                            Po3 = Po[:, 0:GW].rearrange("p (g f) -> p g f", g=Gs)
                            if d == 0:
                                cp("act", opart[:, b0:b0 + Gs, :], Po3, [bPo], [B_op])
                            else:
                                tt("dve", opart[:, b0:b0 + Gs, :], opart[:, b0:b0 + Gs, :], Po3, ALU.add, [B_op, bPo], [B_op])
                        PkA, bPkA = psum()
                        PkB, bPkB = psum()
                        for (cidx, pk_, bpk_) in ((0, PkA, bPkA), (1, PkB, bPkB)):
                            for i in range(Gs):
                                mm(pk_[:, i * 128:(i + 1) * 128], kdzG[:, i, cidx, :], vSG[:, i, :], True, True, [B_kdz, B_v], [bpk_],
                                   inc=(i == Gs - 1))
                            cp("act", KV4[:, b0:b0 + Gs, cidx, :], pk_[:, 0:GW].rearrange("p (g f) -> p g f", g=Gs), [bpk_], [B_KV])
                    ngr_ = nblk // Gs
                    a1(0)
                    for gi in range(ngr_):
                        if gi + 1 < ngr_:
                            a1(gi + 1)
                        a2(gi)
                    order = list(range(nch)) if d == 0 else list(range(nch - 1, -1, -1))
                    if not is_ctx:
                        k.op("dve", lambda e: e.memset(PcJ[:, order[0]:order[0] + 1], 1.0), [], [B_Pc])
                    for p_, j in enumerate(order):
                        if p_ > 0:
                            jp = order[p_ - 1]
                            stt("dve", KVs[:, j, :], KVs[:, jp, :], Acol[:, j:j + 1], KVs[:, j, :], ALU.mult, ALU.add,
                                [B_KV, B_A], [B_KV])
                        if not is_ctx:
                            dst = PcJ[:, order[p_ + 1]:order[p_ + 1] + 1] if p_ + 1 < nch else Pend[:]
                            ts("dve", dst, PcJ[:, j:j + 1], Acol[:, j:j + 1], None, ALU.mult, None, [B_Pc, B_A], [B_Pc])
                    if is_ctx:
                        return order[-1]
                    for q4 in range(0, nch, 8):
                        n4 = min(8, nch - q4)
                        cp("act", Sbf[:, q4:q4 + n4, :], KVs[:, q4:q4 + n4, :], [B_KV], [B_Sbf])
                    for gi in range(nblk // Gs):
                        b0 = gi * Gs
                        GW = Gs * 128
                        PcA, bA = psum()
                        PcB, bB = psum()
                        for (cidx, pc_, bpc_) in ((0, PcA, bA), (1, PcB, bB)):
                            for i in range(Gs):
                                tb = b0 + i
                                j = 2 * tb + cidx
                                sidx = (j - 1) if d == 0 else (j + 1)
                                st_ = Sbf[:, sidx, :] if 0 <= sidx < nch else Zs[:]
                                mm(pc_[:, i * 128:(i + 1) * 128], qdT[:, d, tb * 128:(tb + 1) * 128], st_, True, True,
                                   [B_qd[d], B_Sbf], [bpc_], inc=(i == Gs - 1))
                        tt("dve", opart[0:64, b0:b0 + Gs, :], opart[0:64, b0:b0 + Gs, :],
                           PcA[0:64, 0:GW].rearrange("p (g f) -> p g f", g=Gs), ALU.add, [B_op, bA], [B_op])
                        tt("dve", opart[64:128, b0:b0 + Gs, :], opart[64:128, b0:b0 + Gs, :],
                           PcB[64:128, 0:GW].rearrange("p (g f) -> p g f", g=Gs), ALU.add, [B_op, bB], [B_op])
                    qd4 = qdT[:, d, :].rearrange("p (j t) -> p j t", t=64)
                    tt("pool", qd4, qd4, PcJ[:, 0:nch].unsqueeze(2).to_broadcast([128, nch, 64]), ALU.mult, [B_qd[d], B_Pc], [B_qd[d]])
                    return order[-1]

                fetch_q(0)
                fetch_wh(0, 0)
                def prep_head(h):
                    for sl in range(2):
                        for dd in range(2):
                            k.dma("sp", c_misc, lambda e, sl=sl, dd=dd: e.dma_start(
                                out=lbt[:, sl, dd, :], in_=bcast(lb_d, 128, (sl * 2 + dd) * AW + h * 128)), writes=[B_lb])
                    tt("dve", lbv[:, 0, :, :], lbt[:, 1, :, :], lbt[:, 0, :, :], ALU.subtract, [B_lb], [B_lb])
                    act(lbv[:, 0, :, :], lbv[:, 0, :, :], AF.Exp, [B_lb], [B_lb])
                    sig_inplace(lbv[:, 0, :, :], B_lb)
                    ts("dve", lbv[:, 1, :, :], lbv[:, 0, :, :], -1.0, 1.0, ALU.mult, ALU.add, [B_lb], [B_lb])
                    TGq = min(512, TL)
                    etmp = ebG[:].rearrange("p g f -> p (g f)")
                    for tg in range(TL // TGq):
                        pq, bq = psum()
                        for c in range(KD):
                            mm(pq[:, 0:TGq], whq[:, c, :], hxT[:, c, tg * TGq:(tg + 1) * TGq], c == 0, c == KD - 1,
                               [B_whq] + B_hx, [bq], inc=(c == KD - 1))
                        act(etmp[:, 0:TGq], pq[:, 0:TGq], AF.Exp, [bq], [B_eb], scale=-1.0)
                        sig_inplace(etmp[:, 0:TGq], B_eb)
                        tt("dve", qsT[:, tg * TGq:(tg + 1) * TGq], pq[:, 0:TGq], etmp[:, 0:TGq], ALU.mult, [bq, B_eb], [B_qs])
                    if h + 1 < NH:
                        fetch_q(h + 1)

                prep_head(0)
                for h in range(NH):
                    for d in range(2):
                        wh, B_wh = whs[(2 * h + d) % 2]
                        if d == 0:
                            fetch_wh(h, 1)
                        elif h + 1 < NH:
                            fetch_wh(h + 1, 0)
                        jl = sweep(h, d, hcT, B_hc, NCB, True)
                        cp("pool", ctxS_t[d][:], KVs[:, jl, :], [B_KV], [B_tmp])
                        jl = sweep(h, d, hxT, B_hx, NB, False)
                        cp("pool", summ[:, d, 0:128], KVs[:, jl, :], [B_KV], [B_sum])
                        cp("pool", summ[:, d, 128:129], Pend[:], [B_Pc], [B_sum])
                    B_si = Buf("stin")
                    B_so = Buf("stout")
                    k.dma("sp", c_misc, lambda e, h=h: e.dma_start(out=stin_d[h].ap(), in_=summ[:].rearrange("p d f -> p (d f)")),
                          reads=[B_sum], writes=[B_si])
                    k.cc(c_cc, lambda e, h=h: e.collective_compute(
                        "AllGather", ALU.bypass, replica_groups=[list(range(NCORES))],
                        ins=[stin_d[h].ap().opt()], outs=[stout_d[h].ap().opt()]), reads=[B_si], writes=[B_so])
                    if h + 1 < NH:
                        prep_head(h + 1)
                    for d in range(2):
                        k.dma("sp", c_misc, lambda e, h=h, d=d: e.dma_start(
                            out=agT[:], in_=stout_d[h].ap()[:, d * 129:(d + 1) * 129].rearrange("(r p) f -> p r f", p=128)),
                            reads=[B_so], writes=[B_ag])
                        acc = ctxS_t[d]
                        rr = range(NCORES) if d == 0 else range(NCORES - 1, -1, -1)
                        tt("dve", pm8[:], agT[:, :, 128], cinfo[:, d * 8:d * 8 + 8], ALU.mult, [B_ag, B_ci], [B_pm8])
                        tt("dve", pm8[:], pm8[:], cinfo[:, 16 + d * 8:16 + d * 8 + 8], ALU.add, [B_pm8, B_ci], [B_pm8])
                        tt("pool", agT[:, :, 0:128], agT[:, :, 0:128],
                           cinfo[:, d * 8:d * 8 + 8].unsqueeze(2).to_broadcast([128, NCORES, 128]), ALU.mult, [B_ag, B_ci], [B_ag])
                        for r in rr:
                            stt("dve", acc[:], acc[:], pm8[:, r:r + 1], agT[:, r, 0:128], ALU.mult, ALU.add, [B_tmp, B_pm8, B_ag], [B_tmp])
                        cp("act", Sin[:, d, :], acc[:], [B_tmp], [B_Sin])
                    oSG, B_oS = fG, B_f
                    yaG, B_yaG = lgG, B_lg
                    for gi in range(NB // G):
                        b0 = gi * G
                        GW = G * 128
                        Pp, bPp = psum()
                        for i in range(G):
                            tb = b0 + i
                            mm(Pp[:, i * 128:(i + 1) * 128], qdT[:, 0, tb * 128:(tb + 1) * 128], Sin[:, 0, :], True, False,
                               [B_qd[0], B_Sin], [bPp], inc=False)
                            mm(Pp[:, i * 128:(i + 1) * 128], qdT[:, 1, tb * 128:(tb + 1) * 128], Sin[:, 1, :], False, True,
                               [B_qd[1], B_Sin], [bPp], inc=(i == G - 1))
                        tt("dve", oSG[:], Pp[:, 0:GW].rearrange("p (g f) -> p g f", g=G), opart[:, b0:b0 + G, :], ALU.add,
                           [bPp, B_op], [B_oS])
                        for i in range(G):
                            act(yaG[:, i, :], oSG[:, i, :], AF.Square, [B_oS], [B_yaG, B_ss], accum_out=ssG[:, i:i + 1])
                        rstd_from(ssG[:], 128, B_ss)
                        for i in range(G):
                            stt("dve", yaG[:, i, :], oSG[:, i, :], ssG[:, i:i + 1], ggS[:, b0 + i, :], ALU.mult, ALU.mult,
                                [B_oS, B_ss, B_gg, B_yaG], [B_yaG])
                        Pt, bPt = psum()
                        for i in range(G):
                            tr(Pt[:, i * 128:(i + 1) * 128], yaG[:, i, :], [B_yaG], [bPt], inc=(i == G - 1))
                        yst, byst = ystg[ysi[0]]
                        ysi[0] = 1 - ysi[0]
                        cp("act", yst[:, 0:GW], Pt[:, 0:GW], [bPt], [byst])
                        k.dma("sp", c_ya, lambda e, yst=yst, h=h, b0=b0, GW=GW: e.dma_start(
                            out=ya_d.ap()[:, h, b0 * 128:b0 * 128 + GW], in_=yst[:, 0:GW]), reads=[byst], writes=[B_ya])
                k.barrier()
            if stop == "S2":
                k.dead = True
            sHX.close()

            TH = min(512, TL)
            NQ = TL // TH
            nbq = TH // 128
            B_xmid = [Buf(f"xmid{i}") for i in range(NB)]
            B_hbin = Buf("hbin")
            with ExitStack() as s3:
                set_stg(s3, 3, 1024)
                dualq[0] = True
                xload3, norm_T3, _, _ = mk_norm(s3, nbq)
                wtile = mk_wring(s3)
                g1bc = k.sb("g1bc", [128, D], F32, s3)
                B_gbc = Buf("gbc")
                dg = k.sb("dg", [128, 128], F32, s3)
                B_dg = Buf("dg")
                for c4 in range(4):
                    pt, pb = psum()
                    for i in range(4):
                        c = c4 * 4 + i
                        ts("dve", dg[:], ident, modv[:, 32 + c, 0:1], None, ALU.mult, None, [B_cst, B_mod], [B_dg])
                        mm(pt[:, i * 128:(i + 1) * 128], ones, dg[:], True, True, [B_cst, B_dg], [pb])
                    cp("act", g1bc[:, c4 * 512:(c4 + 1) * 512], pt[:], [pb], [B_gbc])
                hxq = k.sb("hxq", [128, KD, TH + 2], BF16, s3)
                B_hq = Buf("hxq")
                hal2 = k.sb("hal2", [128, D], F32, s3)
                B_hl2 = Buf("hal2")
                c_hl2 = k.newctr("ld_hal2")
                k.op("pool", lambda e: e.memset(hal2[:], 0.0), [], [B_hl2])
                yaq = k.sb("yaq", [128, NH, TH], BF16, s3)
                B_yaq = Buf("yaq")
                c_yaq = k.newctr("ld_yaq")
                ybT = k.sb("ybT", [128, 8, TH], BF16, s3)
                B_yb = Buf("ybT")
                mgT = k.sb("mgT", [128, KD, TH], BF16, s3)
                B_mg = Buf("mgT")
                ws3s = [(k.sb(f"ws3_{i}", [128, KD, 384], BF16, s3), Buf(f"ws3_{i}")) for i in range(2)]
                wp3s = [(k.sb(f"wp3_{i}", [128, 8, 256], BF16, s3), Buf(f"wp3_{i}")) for i in range(2)]
                jobs3 = [("sc", cc) for cc in range(8)] + [("mg", m) for m in range(KD)]

                def fetch_job(ji):
                    kind, i_ = jobs3[ji]
                    wsb, bws = ws3s[ji % 2]
                    if kind == "sc":
                        for pi, c0 in enumerate((5 * AW + i_ * 128, 6 * AW + i_ * 128, 7 * AW + i_ * 128)):
                            fetch(wsb[:, :, pi * 128:(pi + 1) * 128], bws, wsrc(win_d, 0, KD, c0, 128), KD, 128)
                    else:
                        for pi, c0 in enumerate((8 * AW + i_ * 128, 10 * AW + i_ * 128)):
                            fetch(wsb[:, :, pi * 128:(pi + 1) * 128], bws, wsrc(win_d, 0, KD, c0, 128), KD, 128)
                        wpb_t, bwp = wp3s[ji % 2]
                        fetch(wpb_t[:, :, 0:128], bwp, wsrc(wpa_d, 0, 8, i_ * 128, 128), 8, 128)
                        fetch(wpb_t[:, :, 128:256], bwp, wsrc(wpb_d, 0, 8, i_ * 128, 128), 8, 128)
                uext = k.sb("uext", [128, TH + 2], F32, s3)
                B_u = Buf("uext")
                scS = k.sb("scS", [128, 512], F32, s3)
                B_sc = Buf("scS")
                accS = k.sb("accS", [128, TH], F32, s3)
                B_acc = Buf("accS")
                sgS = [(k.sb(f"sgS{i}", [128, 512], F32, s3), Buf(f"sgS{i}")) for i in range(2)]
                t12 = [(k.sb(f"t12{i}", [128, 512], F32, s3), Buf(f"t12{i}")) for i in range(2)]
                tmpx = k.sb("tmpx", [128, 512], F32, s3)
                B_tx = Buf("tmpx")
                for q in range(NQ):
                    t0 = q * TH
                    xq = []
                    for tbl in range(nbq):
                        tbg = t0 // 128 + tbl
                        xt, xb = xload3(x_d.ap()[tbg * 128:(tbg + 1) * 128, :])
                        norm_T3(xt, xb, 0, sh1, hxq, B_hq, [(tbl * 128, 128, 0)])
                        xq.append((xt, xb))
                    k.dma("sp", c_yaq, lambda e, t0=t0: e.dma_start(out=yaq[:], in_=ya_d.ap()[:, :, t0:t0 + TH]),
                          reads=[B_ya], writes=[B_yaq])
                    lsrc = x_d.ap()[t0 - 1:t0, :] if t0 > 0 else xh_d.ap()[0:1, :]
                    rsrc = x_d.ap()[t0 + TH:t0 + TH + 1, :] if t0 + TH < TL else xh_d.ap()[1:2, :]
                    k.dma("sp", c_hl2, lambda e, lsrc=lsrc: e.dma_start(out=hal2[0:1, :], in_=lsrc), writes=[B_hl2])
                    k.dma("sp", c_hl2, lambda e, rsrc=rsrc: e.dma_start(out=hal2[1:2, :], in_=rsrc), writes=[B_hl2])
                    norm_T3(hal2, B_hl2, 0, sh1, hxq, B_hq, [(TH, 2, 0)])
                    lmask = cinfo[:, 32:33] if t0 == 0 else None
                    rmask = cinfo[:, 33:34] if t0 + TH == TL else None
                    fetch_job(0)
                    for cc in range(8):
                        fetch_job(cc + 1)
                        ws3, B_ws3 = ws3s[cc % 2]
                        pcs, bcs = psum()
                        phs, bhs = psum()
                        pbs, bbs = psum()
                        ph, bh = psum()
                        for (pp, bb_, pi) in ((pcs, bcs, 1), (phs, bhs, 2), (pbs, bbs, 0)):
                            for c in range(KD):
                                mm(pp[:, 0:TH], ws3[:, c, pi * 128:(pi + 1) * 128], hxq[:, c, 0:TH], c == 0, c == KD - 1,
                                   [B_ws3, B_hq], [bb_], inc=(c == KD - 1))
                        for pi in (1, 2):
                            for c in range(KD):
                                mm(ph[:, (pi - 1) * 2:(pi - 1) * 2 + 2], ws3[:, c, pi * 128:(pi + 1) * 128], hxq[:, c, TH:TH + 2],
                                   c == 0, c == KD - 1, [B_ws3, B_hq], [bh])
                        cp("act", scS[:, 0:TH], pcs[:, 0:TH], [bcs], [B_sc])
                        tt("dve", uext[:, 1:TH + 1], scS[:, 0:TH], phs[:, 0:TH], ALU.mult, [B_sc, bhs], [B_u])
                        cp("act", scS[:, 0:2], ph[:, 0:2], [bh, B_u], [B_sc])
                        tt("dve", uext[:, 0:1], scS[:, 0:1], ph[:, 2:3], ALU.mult, [B_sc, bh], [B_u])
                        tt("dve", uext[:, TH + 1:TH + 2], scS[:, 1:2], ph[:, 3:4], ALU.mult, [B_sc, bh], [B_u])
                        if lmask is not None:
                            ts("dve", uext[:, 0:1], uext[:, 0:1], lmask, None, ALU.mult, None, [B_u, B_ci], [B_u])
                        if rmask is not None:
                            ts("dve", uext[:, TH + 1:TH + 2], uext[:, TH + 1:TH + 2], rmask, None, ALU.mult, None, [B_u, B_ci], [B_u])
                        ts("dve", accS[:], uext[:, 0:TH], swt[:, cc, 0:1], None, ALU.mult, None, [B_u, B_small], [B_acc])
                        stt("dve", accS[:], uext[:, 1:TH + 1], swt[:, cc, 1:2], accS[:], ALU.mult, ALU.add, [B_u, B_small, B_acc], [B_acc])
                        stt("dve", accS[:], uext[:, 2:TH + 2], swt[:, cc, 2:3], accS[:], ALU.mult, ALU.add, [B_u, B_small, B_acc], [B_acc])
                        tt("dve", ybT[:, cc, :], accS[:], pbs[:, 0:TH], ALU.mult, [B_acc, bbs], [B_yb])
                    for m in range(KD):
                        if 8 + m + 1 < len(jobs3):
                            fetch_job(8 + m + 1)
                        ws3, B_ws3 = ws3s[(8 + m) % 2]
                        wp3, B_wp3 = wp3s[(8 + m) % 2]
                        pga, bga = psum()
                        pgb, bgb = psum()
                        ppa, bpa = psum()
                        ppb, bpb = psum()
                        for c in range(KD):
                            mm(pga[:, 0:TH], ws3[:, c, 0:128], hxq[:, c, 0:TH], c == 0, c == KD - 1, [B_ws3, B_hq], [bga], inc=(c == KD - 1))
                        for c in range(KD):
                            mm(pgb[:, 0:TH], ws3[:, c, 128:256], hxq[:, c, 0:TH], c == 0, c == KD - 1, [B_ws3, B_hq], [bgb], inc=(c == KD - 1))
                        for c in range(8):
                            mm(ppa[:, 0:TH], wp3[:, c, 0:128], yaq[:, c, :], c == 0, c == 7, [B_wp3, B_yaq], [bpa], inc=(c == 7))
                        for c in range(8):
                            mm(ppb[:, 0:TH], wp3[:, c, 128:256], ybT[:, c, :], c == 0, c == 7, [B_wp3, B_yb], [bpb], inc=(c == 7))
                        (sa, bsa), (sb_, bsb) = sgS
                        (ta, bta), (tb_, btb) = t12
                        act(sa[:, 0:TH], pga[:, 0:TH], AF.Sigmoid, [bga], [bsa])
                        act(sb_[:, 0:TH], pgb[:, 0:TH], AF.Sigmoid, [bgb], [bsb])
                        tt("dve", ta[:, 0:TH], ppa[:, 0:TH], sa[:, 0:TH], ALU.mult, [bpa, bsa], [bta])
                        tt("dve", tb_[:, 0:TH], ppb[:, 0:TH], sb_[:, 0:TH], ALU.mult, [bpb, bsb], [btb])
                        tt("pool", mgT[:, m, :], ta[:, 0:TH], tb_[:, 0:TH], ALU.add, [bta, btb], [B_mg])
                    for ng in range(4):
                        pp = [psum() for _ in range(nbq)]
                        for c2 in range(KD // 2):
                            wv, wb_ = wtile(wo_d, c2 * 2, 2, ng * 512, 512)
                            for i in range(nbq):
                                for ci in range(2):
                                    c = c2 * 2 + ci
                                    mm(pp[i][0][:], mgT[:, c, i * 128:(i + 1) * 128], wv[:, ci, :], c == 0, c == KD - 1,
                                       [B_mg, wb_], [pp[i][1]], inc=(c == KD - 1 or (i == nbq - 1 and ci == 1)))
                        for i in range(nbq):
                            xt, xb = xq[i]
                            tt("dve", tmpx[:], pp[i][0][:], g1bc[:, ng * 512:(ng + 1) * 512], ALU.mult, [pp[i][1], B_gbc], [B_tx])
                            tt("pool", xt[:, ng * 512:(ng + 1) * 512], xt[:, ng * 512:(ng + 1) * 512], tmpx[:], ALU.add, [xb, B_tx], [xb])
                    for i in range(nbq):
                        xt, xb = xq[i]
                        tbg = t0 // 128 + i
                        k.dma("sp", c_st, lambda e, xt=xt, tbg=tbg: e.dma_start(
                            out=xmid_d.ap()[tbg * 128:(tbg + 1) * 128, :], in_=xt[:]), reads=[xb], writes=[B_xmid[tbg]])
                        if tbg == 0:
                            k.dma("sp", c_st, lambda e, xt=xt: e.dma_start(out=hbin_d.ap()[0:64, :], in_=xt[0:64, :]),
                                  reads=[xb], writes=[B_hbin])
                        if tbg == NB - 1:
                            k.dma("sp", c_st, lambda e, xt=xt: e.dma_start(out=hbin_d.ap()[64:128, :], in_=xt[64:128, :]),
                                  reads=[xb], writes=[B_hbin])
                k.barrier()
            if stop == "S3":
                k.dead = True
            B_hbo = Buf("hbout")
            k.cc(c_cc, lambda e: e.collective_compute("AllGather", ALU.bypass, replica_groups=[list(range(NCORES))],
                                                      ins=[hbin_d.ap().opt()], outs=[hbout_d.ap().opt()]),
                 reads=[B_hbin], writes=[B_hbo])

            if stop == "S4":
                k.dead = True
            TF = min(512, TL)
            NT = TL // TF
            TE = TF + 128
            nbt = TF // 128
            with ExitStack() as s5:
                set_stg(s5, 4, 1024)
                xload5, norm_T5, xr5, (fj, B_fj) = mk_norm(s5, 4)
                wtile = mk_wring(s5)
                xnS5 = None
                g2bc = k.sb("g2bc", [128, D], F32, s5)
                fgbc = k.sb("fgbc", [128, D], F32, s5)
                B_gbc = Buf("gbc5")
                k.dma("sp", c_misc, lambda e: e.dma_start(out=fgbc[:], in_=bcast(fg_d, D)), writes=[B_gbc])
                dg = k.sb("dg5", [128, 128], F32, s5)
                B_dg = Buf("dg5")
                for c4 in range(4):
                    pt, pb = psum()
                    for i in range(4):
                        c = c4 * 4 + i
                        ts("dve", dg[:], ident, modv[:, 80 + c, 0:1], None, ALU.mult, None, [B_cst, B_mod], [B_dg])
                        mm(pt[:, i * 128:(i + 1) * 128], ones, dg[:], True, True, [B_cst, B_dg], [pb])
                    cp("act", g2bc[:, c4 * 512:(c4 + 1) * 512], pt[:], [pb], [B_gbc])
                xhc = k.sb("xhc", [128, D], F32, s5)
                B_xhc = Buf("xhc")
                k.op("pool", lambda e: e.memset(xhc[:], 0.0), [], [B_xhc])
                for r in range(NCORES):
                    xt, xb, cx = xr5[r % 4]
                    k.dma("sp", cx, lambda e, xt=xt, r=r: e.dma_start(out=xt[0:64, :], in_=hbout_d.ap()[r * 128 + 64:r * 128 + 128, :]),
                          reads=[B_hbo], writes=[xb])
                    k.dma("sp", cx, lambda e, xt=xt, r=r: e.dma_start(out=xt[64:128, :], in_=hbout_d.ap()[r * 128:r * 128 + 64, :]),
                          reads=[B_hbo], writes=[xb])
                    stt("dve", xhc[:], xt[:], cinfo[:, 34 + r:35 + r], xhc[:], ALU.mult, ALU.add, [xb, B_ci, B_xhc], [B_xhc])
                h2T = k.sb("h2T", [128, KD, TE], BF16, s5)
                B_h2 = Buf("h2T")
                actT = k.sb("actT", [128, NFC, TF], BF16, s5)
                B_at = [Buf(f"actT{i}") for i in range(NFC)]
                wu = [(k.sb(f"wu{i}", [128, KD, 256], BF16, s5), Buf(f"wu{i}")) for i in range(3)]
                aS = [(k.sb(f"aS{i}", [128, 3, TE + 2], F32, s5), Buf(f"aS{i}")) for i in range(2)]
                cacc = [(k.sb(f"cacc{i}", [128, TF], F32, s5), Buf(f"cacc{i}")) for i in range(2)]
                cacc2 = [(k.sb(f"caccx{i}", [128, TF], F32, s5), Buf(f"caccx{i}")) for i in range(2)]
                cmask = k.sb("cmask", [128, 2, 66], F32, s5)
                B_cm = Buf("cmask")
                k.dma("sp", c_misc, lambda e: e.dma_start(out=cmask[:], in_=cmask_d.ap()[:, :, 0:66]), writes=[B_cm])
                for a3, ba in aS:
                    k.op("pool", lambda e, a3=a3: e.memset(a3[:], 0.0), [], [ba])
                tmpo, B_to = cacc2[0]
                fss = k.sb("fss", [128, 1], F32, s5)
                B_fss = Buf("fss")
                egrp = []
                o_ = 0
                while o_ < TE:
                    n_ = min(512, TE - o_)
                    egrp.append((o_, n_))
                    o_ += n_
                for it in range(NT):
                    t0 = it * TF
                    for tbl in range(nbt):
                        tbg = t0 // 128 + tbl
                        xt, xb = xload5(xmid_d.ap()[tbg * 128:(tbg + 1) * 128, :])
                        norm_T5(xt, xb, 2, sh2, h2T, B_h2, [(64 + tbl * 128, 128, 0)])
                    hal, B_hal, c_hal = xr5[0]
                    if t0 > 0:
                        k.dma("sp", c_hal, lambda e, t0=t0, hal=hal: e.dma_start(out=hal[0:64, :], in_=xmid_d.ap()[t0 - 64:t0, :]),
                              reads=B_xmid, writes=[B_hal])
                    else:
                        cp("pool", hal[0:64, :], xhc[0:64, :], [B_xhc], [B_hal])
                    if t0 + TF < TL:
                        k.dma("sp", c_hal, lambda e, t0=t0, hal=hal: e.dma_start(out=hal[64:128, :], in_=xmid_d.ap()[t0 + TF:t0 + TF + 64, :]),
                              reads=B_xmid, writes=[B_hal])
                    else:
                        cp("pool", hal[64:128, :], xhc[64:128, :], [B_xhc], [B_hal])
                    norm_T5(hal, B_hal, 2, sh2, h2T, B_h2, [(0, 64, 0), (64 + TF, 64, 64)])
                    pend = []

                    def fetch_issue(fc_):
                        wv_, wbb_ = wu[(it * NFC + fc_) % 3]
                        j_ = 0
                        for (c0_, o_) in ((fc_ * 128, 0), (DFF + fc_ * 128, 128)):
                            for hk in range(2):
                                t_, b_, c_ = stg[stgi[0]]
                                stgi[0] = (stgi[0] + 1) % len(stg)
                                sv_ = t_[:, 0:1024].rearrange("p (c n) -> p c n", c=8)
                                src_ = wsrc(wup_d, hk * 8, 8, c0_, 128)
                                k.dma(("sp", "act")[j_ % 2], c_, lambda e, sv_=sv_, src_=src_: e.dma_start(out=sv_, in_=src_), writes=[b_])
                                pend.append((wv_[:, hk * 8:(hk + 1) * 8, o_:o_ + 128], wbb_, sv_, b_))
                                j_ += 1

                    def fetch_cast():
                        for (dst_, dbuf_, sv_, b_) in pend:
                            cp("act", dst_, sv_, [b_], [dbuf_])
                        del pend[:]
                    fetch_issue(0)
                    fetch_cast()
                    fetch_issue(1)
                    fetch_cast()
                    pbbs = {}

                    def st1(fc):
                        wv, wb_ = wu[(it * NFC + fc) % 3]
                        a3, ba = aS[fc % 2]
                        for (o_, n_) in egrp:
                            pa, bpa = psum()
                            for c in range(KD):
                                mm(pa[:, 0:n_], wv[:, c, 0:128], h2T[:, c, o_:o_ + n_], c == 0, c == KD - 1, [wb_, B_h2], [bpa], inc=(c == KD - 1))
                            cp("act", a3[:, 0, 1 + o_:1 + o_ + n_], pa[:, 0:n_], [bpa], [ba])
                        pbb, bpb = psum()
                        for c in range(KD):
                            mm(pbb[:, 0:TF], wv[:, c, 128:256], h2T[:, c, 64:64 + TF], c == 0, c == KD - 1, [wb_, B_h2], [bpb], inc=(c == KD - 1))
                        pbbs[fc] = (pbb, bpb)

                    def st2(fc):
                        a3, ba = aS[fc % 2]
                        ce = "pool"
                        if t0 == 0:
                            ts(ce, a3[:, 0, 1:65], a3[:, 0, 1:65], cinfo[:, 32:33], None, ALU.mult, None, [ba, B_ci], [ba])
                        if t0 + TF >= TL:
                            ts(ce, a3[:, 0, 65 + TF:129 + TF], a3[:, 0, 65 + TF:129 + TF], cinfo[:, 33:34], None, ALU.mult, None, [ba, B_ci], [ba])
                        a3v = lambda r_: a3[:, r_, 1:TE + 1].rearrange("p (g f) -> p g f", f=64)
                        mk_ = lambda r_: cmask[:, r_, 1:65].unsqueeze(1).to_broadcast([128, TE // 64, 64])
                        tt(ce, a3v(1), a3v(0), mk_(0), ALU.mult, [ba, B_cm], [ba])
                        tt(ce, a3v(2), a3v(0), mk_(1), ALU.mult, [ba, B_cm], [ba])
                        ca, bca = cacc[fc % 2]
                        accs = [(ca, bca), (cacc2[0][0], cacc2[0][1]), (cacc2[1][0], cacc2[1][1])]
                        for dx in (-1, 0, 1):
                            for dyi, dy in enumerate((-1, 0, 1)):
                                e0 = 65 + 64 * dy + dx
                                src = a3[:, (1, 0, 2)[dx + 1], e0:e0 + TF]
                                ti = (dy + 1) * 3 + (dx + 1)
                                wcol = dwt[:, fc, ti:ti + 1]
                                at_, ab_ = accs[dyi]
                                if dx == -1:
                                    ts("dve", at_[:], src, wcol, None, ALU.mult, None, [ba, B_small], [ab_])
                                else:
                                    stt("dve", at_[:], src, wcol, at_[:], ALU.mult, ALU.add, [ba, B_small, ab_], [ab_])
                        tt("pool", accs[1][0][:], accs[1][0][:], accs[2][0][:], ALU.add, [accs[1][1], accs[2][1]], [accs[1][1]])
                        tt("dve", ca[:], ca[:], accs[1][0][:], ALU.add, [bca, accs[1][1]], [bca])

                    def st3(fc):
                        ca, bca = cacc[fc % 2]
                        pbb, bpb = pbbs.pop(fc)
                        act(ca[:], ca[:], AF.Silu, [bca, B_small], [bca], bias=dbt[:, fc:fc + 1])
                        tt("dve", actT[:, fc, :], ca[:], pbb[:, 0:TF], ALU.mult, [bca, bpb], [B_at[fc]])

                    for i_ in range(NFC + 2):
                        if i_ + 2 < NFC:
                            fetch_issue(i_ + 2)
                        if i_ < NFC:
                            st1(i_)
                        if 0 <= i_ - 1 < NFC:
                            st2(i_ - 1)
                        if 0 <= i_ - 2 < NFC:
                            st3(i_ - 2)
                        fetch_cast()
                    xo = []
                    for i in range(nbt):
                        tbg = t0 // 128 + i
                        xo.append(xload5(xmid_d.ap()[tbg * 128:(tbg + 1) * 128, :]))
                    for ng in range(4):
                        pp = [psum() for _ in range(nbt)]
                        for k2 in range(NFC // 2):
                            wv, wb_ = wtile(wdn_d, k2 * 2, 2, ng * 512, 512)
                            for i in range(nbt):
                                for ci in range(2):
                                    kc = k2 * 2 + ci
                                    mm(pp[i][0][:], actT[:, kc, i * 128:(i + 1) * 128], wv[:, ci, :], kc == 0, kc == NFC - 1,
                                       [B_at[kc], wb_], [pp[i][1]], inc=(kc == NFC - 1 or (i == nbt - 1 and ci == 1)))
                        for i in range(nbt):
                            xt, xb = xo[i]
                            tt("dve", tmpo[:], pp[i][0][:], g2bc[:, ng * 512:(ng + 1) * 512], ALU.mult, [pp[i][1], B_gbc], [B_to])
                            tt("pool", xt[:, ng * 512:(ng + 1) * 512], xt[:, ng * 512:(ng + 1) * 512], tmpo[:], ALU.add, [xb, B_to], [xb])
                    for i in range(nbt):
                        xt, xb = xo[i]
                        tbg = t0 // 128 + i
                        act(fj[:], xt[:], AF.Square, [xb], [B_fj, B_fss], accum_out=fss[:])
                        rstd_from(fss[:], D, B_fss)
                        stt("dve", xt[:], xt[:], fss[:], fgbc[:], ALU.mult, ALU.mult, [xb, B_fss, B_gbc], [xb])
                        k.dma("sp", c_st, lambda e, xt=xt, tbg=tbg: e.dma_start(out=out_d.ap()[tbg * 128:(tbg + 1) * 128, :], in_=xt[:]),
                              reads=[xb], writes=[Buf("outrow")])

        except _Stop:
            raise
        k.dead = False
        k.barrier()
        k.emit()
    return nc


def _consts():
    c = np.zeros((128, 7, 128), np.float32)
    i = np.arange(128)
    same = (i[:, None] // 64) == (i[None, :] // 64)
    c[:, 0, :] = np.eye(128)
    c[:, 1, :] = same & (i[:, None] <= i[None, :])
    c[:, 2, :] = same & (i[:, None] >= i[None, :])
    c[:, 3, :] = same & (i[:, None] > i[None, :])
    c[:, 4, :] = same & (i[:, None] < i[None, :])
    c[:, 5, :] = 1.0
    return c


def _fp(v):
    return np.ascontiguousarray(v.reshape(-1, 128).T)


def make_in_maps(TL, x, c, ctx, c_ctx, w_mod, b_mod, norm1_g, w_in, lb_raw, a_norm_g, sconv_w,
                 w_pa, w_pb, w_o, norm2_g, w_up, ffn_dw, ffn_db, w_down, final_g):
    f = lambda a: np.ascontiguousarray(np.asarray(a, dtype=np.float32))
    x2 = f(x)[0]
    S = x2.shape[0]
    assert S == TL * NCORES
    wm = f(w_mod)[0]
    bmodv = f(b_mod)[0]
    cvec = np.stack([_fp(f(c)[0]), _fp(f(c_ctx))], axis=-1)
    cmask = np.ones((128, 2, 1200), np.float32)
    e = np.arange(1200)
    gcol = (e - 1) % 64
    cmask[:, 0, gcol == 63] = 0.0
    cmask[:, 1, gcol == 0] = 0.0
    shared = {
        "ctx": f(ctx)[0], "cvec": np.ascontiguousarray(cvec), "n1g": _fp(f(norm1_g)[0]), "n2g": _fp(f(norm2_g)[0]),
        "fg": f(final_g).reshape(1, D), "w_in": f(w_in)[0], "lbraw": f(lb_raw).reshape(1, -1),
        "ang": f(a_norm_g).reshape(1, 128),
        "sconvw": np.ascontiguousarray(f(sconv_w)[0].reshape(3, 8, 128).transpose(2, 1, 0)),
        "w_pa": f(w_pa)[0], "w_pb": f(w_pb)[0], "w_o": f(w_o)[0], "w_up": f(w_up)[0],
        "ffndw": np.ascontiguousarray(f(ffn_dw)[0].reshape(9, NFC, 128).transpose(2, 1, 0)),
        "ffndb": _fp(f(ffn_db)[0]), "w_down": f(w_down)[0], "consts": _consts(), "cmask": cmask,
    }
    maps = []
    for r in range(NCORES):
        m = dict(shared)
        m["x"] = np.ascontiguousarray(x2[r * TL:(r + 1) * TL])
        xh = np.zeros((128, D), np.float32)
        if r > 0:
            xh[0] = x2[r * TL - 1]
        if r < NCORES - 1:
            xh[1] = x2[(r + 1) * TL]
        m["xh"] = xh
        m["wmod"] = np.ascontiguousarray(wm[:, r * 1536:(r + 1) * 1536])
        m["bmod"] = _fp(bmodv[r * 1536:(r + 1) * 1536])
        ci = np.zeros((128, 48), np.float32)
        for q in range(NCORES):
            ci[:, q] = 1.0 if q < r else 0.0
            ci[:, 8 + q] = 1.0 if q > r else 0.0
            ci[:, 16 + q] = 1.0 - ci[:, q]
            ci[:, 24 + q] = 1.0 - ci[:, 8 + q]
            ci[0:64, 34 + q] = 1.0 if q == r - 1 else 0.0
            ci[64:128, 34 + q] = 1.0 if q == r + 1 else 0.0
        ci[:, 32] = 1.0 if r > 0 else 0.0
        ci[:, 33] = 1.0 if r < NCORES - 1 else 0.0
        m["cinfo"] = ci
        maps.append(m)
    return maps


_CACHE = {}


def kernel(**inputs):
    S = inputs["x"].shape[1]
    TL = S // NCORES
    if TL not in _CACHE:
        _CACHE[TL] = build_program(TL)
    nc = _CACHE[TL]
    maps = make_in_maps(TL, **inputs)
    res = run_bass_kernel_spmd(nc, maps, core_ids=list(range(NCORES)))
    out = np.concatenate([res.results[r]["out"] for r in range(NCORES)], axis=0)
    return out.reshape(1, S, D).astype(np.float32)
```
